# Optimizing a Trainium2 kernel written in Bass

```python
import math
import jax, jax.numpy as jnp
from jax import lax
import numpy as np

D_MODEL = 2048
BATCH = 16
SEQ = 256
DEPTH = 2
DEC_BATCH = 4
DEC_SEQ = 1024
PAST_LEN = 512

GRID_W = 64
N_MIXERS = 2
N_HYENA = (DEPTH + 1) // 2
N_RET = DEPTH // 2
D_FF = 4 * D_MODEL
EPS = 1e-6
HYENA_ORDER = 2
FILTER_BANDS = 16
FILTER_EMB = 1 + 2 * FILTER_BANDS
FILTER_WIDTH = 64
DECAY_FAST_PCT = 0.3
DECAY_SLOW_PCT = 1.5
DECAY_TARGET = 1e-2
MIN_DECAY = math.log(DECAY_TARGET) / DECAY_SLOW_PCT
MAX_DECAY = math.log(DECAY_TARGET) / DECAY_FAST_PCT
RET_HEADS = 8
RET_DK = D_MODEL // RET_HEADS
RET_DV = 2 * D_MODEL // RET_HEADS
RET_CHUNK = 128
ROPE_BASE = 10000.0

kernel_name = "hyena_retnet_prefix_dit_step"


def _rms_norm(x, g):
    xf = x.astype(jnp.float32)
    y = xf * lax.rsqrt(jnp.mean(xf * xf, axis=-1, keepdims=True) + EPS)
    return (y * g.astype(jnp.float32)).astype(x.dtype)


def _short_conv(x, w, b):
    xp = jnp.pad(x, ((0, 0), (1, 1), (0, 0)))
    return xp[:, :-2] * w[0] + xp[:, 1:-1] * w[1] + xp[:, 2:] * w[2] + b


def _hyena_filters(L, w1, b1, w2, b2, w3, b3, freq, w_out):
    f32 = jnp.float32
    pos = jnp.arange(L, dtype=f32)
    t = pos / L
    omega = 2.0 * math.pi * pos / L
    bands = jnp.linspace(1e-4, FILTER_BANDS - 1, FILTER_BANDS, dtype=f32)
    ang = omega[:, None] * bands[None, :]
    feat = jnp.concatenate([t[:, None], jnp.cos(ang), -jnp.sin(ang)], axis=-1)
    freq = freq.astype(f32)
    h = jnp.sin(freq[0] * (feat @ w1.astype(f32) + b1.astype(f32)))
    h = jnp.sin(freq[1] * (h @ w2.astype(f32) + b2.astype(f32)))
    h = jnp.sin(freq[2] * (h @ w3.astype(f32) + b3.astype(f32)))
    h = h @ w_out.astype(f32)
    deltas = jnp.abs(jnp.linspace(MIN_DECAY, MAX_DECAY, D_MODEL, dtype=f32))
    window = jnp.exp(-t[:, None] * deltas[None, :])
    return h.reshape(L, HYENA_ORDER, 2, D_MODEL) * window[:, None, None, :]


def _long_conv(u, hf, hb, skip):
    L = u.shape[1]
    n = 2 * L
    k = jnp.concatenate([hf, jnp.zeros((1, hf.shape[1]), hf.dtype), hb[:0:-1]], axis=0)
    y = jnp.fft.irfft(jnp.fft.rfft(u, n=n, axis=1) * jnp.fft.rfft(k, n=n, axis=0)[None], n=n, axis=1)[:, :L]
    return y + u * skip.astype(jnp.float32)


def _hyena(h, w_in, b_in, conv_w, conv_b, f_w1, f_b1, f_w2, f_b2, f_w3, f_b3, f_freq, f_wout, f_skip, w_out, b_out):
    L = h.shape[1]
    proj = _short_conv(h @ w_in + b_in, conv_w, conv_b).astype(jnp.float32)
    v, x1, x2 = jnp.split(proj, 3, axis=-1)
    filt = _hyena_filters(L, f_w1, f_b1, f_w2, f_b2, f_w3, f_b3, f_freq, f_wout)
    z = x1 * _long_conv(v, filt[:, 0, 0], filt[:, 0, 1], f_skip[0])
    z = x2 * _long_conv(z, filt[:, 1, 0], filt[:, 1, 1], f_skip[1])
    return z.astype(h.dtype) @ w_out + b_out


def _axial_rotary(x):
    f32 = jnp.float32
    L = x.shape[1]
    n_rows = L // GRID_W
    rows = jnp.repeat(jnp.arange(n_rows, dtype=f32), GRID_W)
    cols = jnp.tile(jnp.arange(GRID_W, dtype=f32), n_rows)
    n_pairs = RET_DK // 4
    inv = ROPE_BASE ** (-jnp.arange(n_pairs, dtype=f32) / n_pairs)
    ang = jnp.concatenate([rows[:, None] * inv, cols[:, None] * inv], axis=-1)
    cos = jnp.cos(ang)[None, :, None, :]
    sin = jnp.sin(ang)[None, :, None, :]
    xp = x.reshape(*x.shape[:-1], RET_DK // 2, 2)
    x0, x1 = xp[..., 0], xp[..., 1]
    return jnp.stack([x0 * cos - x1 * sin, x0 * sin + x1 * cos], axis=-1).reshape(x.shape)


def _retention_scan(q, k, v, log_g, s0):
    B, H, L, _ = q.shape
    C = RET_CHUNK
    n = L // C
    idx = jnp.arange(C, dtype=jnp.float32)
    diff = idx[:, None] - idx[None, :]
    lower = diff >= 0
    intra = jnp.where(lower, jnp.exp(jnp.where(lower, diff, 0.0)[None] * log_g[:, None, None]), 0.0)
    q_decay = jnp.exp((idx[None, :] + 1.0) * log_g[:, None])[..., None]
    k_decay = jnp.exp((C - 1.0 - idx[None, :]) * log_g[:, None])[..., None]
    chunk_decay = jnp.exp(C * log_g)[:, None, None]
    qs = jnp.moveaxis(q.reshape(B, H, n, C, RET_DK), 2, 0)
    ks = jnp.moveaxis(k.reshape(B, H, n, C, RET_DK), 2, 0)
    vs = jnp.moveaxis(v.reshape(B, H, n, C, RET_DV), 2, 0)

    def body(s, qkv):
        qc, kc, vc = qkv
        scores = jnp.einsum('bhid,bhjd->bhij', qc, kc) * intra
        out = jnp.einsum('bhij,bhjv->bhiv', scores, vc) + jnp.einsum('bhid,bhdv->bhiv', qc * q_decay, s)
        s_new = s * chunk_decay + jnp.einsum('bhjd,bhjv->bhdv', kc * k_decay, vc)
        return s_new, out

    s_final, outs = lax.scan(body, s0, (qs, ks, vs))
    return jnp.moveaxis(outs, 0, 2).reshape(B, H, L, RET_DV), s_final


def _retention(h, s0, w_qkvg, decay_logit, gn_g, w_o, grid_positions):
    f32 = jnp.float32
    B, L, _ = h.shape
    q, k, v, g = jnp.split(h @ w_qkvg, [D_MODEL, 2 * D_MODEL, 4 * D_MODEL], axis=-1)
    q = q.reshape(B, L, RET_HEADS, RET_DK).astype(f32)
    k = k.reshape(B, L, RET_HEADS, RET_DK).astype(f32)
    if grid_positions:
        q = _axial_rotary(q)
        k = _axial_rotary(k)
    k = k * RET_DK ** -0.5
    v = v.reshape(B, L, RET_HEADS, RET_DV).astype(f32)
    q, k, v = (jnp.swapaxes(a, 1, 2) for a in (q, k, v))
    log_g = jax.nn.log_sigmoid(decay_logit.astype(f32))
    s0 = s0.astype(f32)
    y_f, s_f = _retention_scan(q, k, v, log_g[0], s0[:, 0])
    y_b, s_b = _retention_scan(jnp.flip(q, 2), jnp.flip(k, 2), jnp.flip(v, 2), log_g[1], s0[:, 1])
    y = jnp.swapaxes(y_f + jnp.flip(y_b, 2), 1, 2)
    mu = jnp.mean(y, axis=-1, keepdims=True)
    var = jnp.mean(jnp.square(y - mu), axis=-1, keepdims=True)
    y = ((y - mu) * lax.rsqrt(var + EPS)).reshape(B, L, 2 * D_MODEL) * gn_g.astype(f32)
    out = (jax.nn.silu(g.astype(f32)) * y).astype(h.dtype) @ w_o
    return out, jnp.stack([s_f, s_b], axis=1)


def _sq_relu_mlp(h, w1, w2):
    return jnp.square(jax.nn.relu(h @ w1)) @ w2


def setup_inputs(seed: int = 0) -> dict:
    key = jax.random.key(seed)
    ks = jax.random.split(key, 32)
    f32 = jnp.float32
    D = D_MODEL

    def nrm(k, shape, s):
        return jax.random.normal(k, shape, f32) * s

    base_logit = jnp.log(2.0 ** (5.0 + jnp.arange(RET_HEADS, dtype=f32)) - 1.0)
    return {
        "x_prompt": nrm(ks[0], (BATCH, SEQ, D), 1.0),
        "x_sample": nrm(ks[1], (DEC_BATCH, DEC_SEQ, D), 1.0),
        "state_ret": nrm(ks[2], (DEC_BATCH, N_RET, 2, RET_HEADS, RET_DK, RET_DV), 0.1),
        "c": nrm(ks[3], (DEC_BATCH, D), 1.0),
        "c_ctx": nrm(ks[4], (D,), 1.0),
        "w_ada": nrm(ks[5], (DEPTH, D, 6 * D), 0.5 * D ** -0.5),
        "b_ada": nrm(ks[6], (DEPTH, 6 * D), 0.02),
        "norm_g": 1.0 + nrm(ks[7], (DEPTH, 2, D), 0.02),
        "final_g": 1.0 + nrm(ks[8], (D,), 0.02),
        "hy_w_in": nrm(ks[9], (N_HYENA, D, 3 * D), D ** -0.5),
        "hy_b_in": nrm(ks[10], (N_HYENA, 3 * D), 0.02),
        "hy_conv_w": nrm(ks[11], (N_HYENA, 3, 3 * D), 0.5),
        "hy_conv_b": nrm(ks[12], (N_HYENA, 3 * D), 0.02),
        "hy_f_w1": nrm(ks[13], (N_HYENA, FILTER_EMB, FILTER_WIDTH), FILTER_EMB ** -0.5),
        "hy_f_b1": nrm(ks[14], (N_HYENA, FILTER_WIDTH), 0.1),
        "hy_f_w2": nrm(ks[15], (N_HYENA, FILTER_WIDTH, FILTER_WIDTH), FILTER_WIDTH ** -0.5),
        "hy_f_b2": nrm(ks[16], (N_HYENA, FILTER_WIDTH), 0.1),
        "hy_f_w3": nrm(ks[17], (N_HYENA, FILTER_WIDTH, FILTER_WIDTH), FILTER_WIDTH ** -0.5),
        "hy_f_b3": nrm(ks[18], (N_HYENA, FILTER_WIDTH), 0.1),
        "hy_f_freq": 1.0 + nrm(ks[19], (N_HYENA, 3, FILTER_WIDTH), 0.01),
        "hy_f_wout": nrm(ks[20], (N_HYENA, FILTER_WIDTH, HYENA_ORDER * 2 * D), 0.1 * FILTER_WIDTH ** -0.5),
        "hy_f_skip": nrm(ks[21], (N_HYENA, HYENA_ORDER, D), 0.5),
        "hy_w_out": nrm(ks[22], (N_HYENA, D, D), D ** -0.5),
        "hy_b_out": nrm(ks[23], (N_HYENA, D), 0.02),
        "ret_w_qkvg": nrm(ks[24], (N_RET, D, 6 * D), D ** -0.5),
        "ret_decay": base_logit[None, None, :] + nrm(ks[25], (N_RET, 2, RET_HEADS), 0.1),
        "ret_gn_g": 1.0 + nrm(ks[26], (N_RET, 2 * D), 0.02),
        "ret_w_o": nrm(ks[27], (N_RET, 2 * D, D), (2 * D) ** -0.5),
        "mlp_w1": nrm(ks[28], (DEPTH, D, D_FF), D ** -0.5),
        "mlp_w2": nrm(ks[29], (DEPTH, D_FF, D), D_FF ** -0.5),
    }


def reference(x_prompt, x_sample, state_ret, c, c_ctx, w_ada, b_ada, norm_g, final_g,
              hy_w_in, hy_b_in, hy_conv_w, hy_conv_b, hy_f_w1, hy_f_b1, hy_f_w2, hy_f_b2,
              hy_f_w3, hy_f_b3, hy_f_freq, hy_f_wout, hy_f_skip, hy_w_out, hy_b_out,
              ret_w_qkvg, ret_decay, ret_gn_g, ret_w_o, mlp_w1, mlp_w2):

    def run(x, cond, s_init, latent):
        cond_act = jax.nn.silu(cond)
        states = []
        for layer in range(DEPTH):
            mod = (cond_act @ w_ada[layer] + b_ada[layer])[:, None, :]
            sh1, sc1, g1, sh2, sc2, g2 = jnp.split(mod, 6, axis=-1)
            h = _rms_norm(x, norm_g[layer, 0]) * (1.0 + sc1) + sh1
            i = layer // N_MIXERS
            if layer % N_MIXERS == 0:
                m = _hyena(h, hy_w_in[i], hy_b_in[i], hy_conv_w[i], hy_conv_b[i],
                           hy_f_w1[i], hy_f_b1[i], hy_f_w2[i], hy_f_b2[i], hy_f_w3[i], hy_f_b3[i],
                           hy_f_freq[i], hy_f_wout[i], hy_f_skip[i], hy_w_out[i], hy_b_out[i])
            else:
                m, s = _retention(h, s_init[:, i], ret_w_qkvg[i], ret_decay[i], ret_gn_g[i], ret_w_o[i], latent)
                states.append(s)
            x = x + g1 * m
            h = _rms_norm(x, norm_g[layer, 1]) * (1.0 + sc2) + sh2
            x = x + g2 * _sq_relu_mlp(h, mlp_w1[layer], mlp_w2[layer])
        return _rms_norm(x, final_g), states

    zero_state = jnp.zeros((x_prompt.shape[0], N_RET, 2, RET_HEADS, RET_DK, RET_DV), jnp.float32)
    y_prompt, ctx_states = run(x_prompt, c_ctx[None, :], zero_state, False)
    y_sample, _ = run(x_sample, c, state_ret, True)
    new_state_ret = jnp.stack(ctx_states, axis=1).astype(x_prompt.dtype)
    return (y_prompt, y_sample, new_state_ret)
```

```python
import os
import numpy as np
import ml_dtypes
import concourse.bass as bass
import concourse.mybir as mybir
from concourse.bass_utils import run_bass_kernel_spmd

F32 = mybir.dt.float32
BF16 = mybir.dt.bfloat16
AF = mybir.ActivationFunctionType
ALU = mybir.AluOpType

D = 2048
T = 1024
NDC = 16
DFF = 8192
EPS = 1e-6
NSLOT = 3
SLOT_ELEMS = 4096
SAME_ENGINE_SYNC = True


class Sched:
    def __init__(self, nc, stack):
        self.nc = nc
        self.stack = stack
        self.prog = {k: [] for k in ("pe", "act", "dve", "pool", "sp")}
        self.tick = {k: 0 for k in self.prog}
        self.waited = {k: {} for k in self.prog}
        self.sems = {}
        for k in self.prog:
            self.sems[k] = stack.enter_context(nc.semaphore("sem_" + k))
        self.state = {}
        self.dcount = {}
        self.nwaits = 0

    def _st(self, key):
        s = self.state.get(key)
        if s is None:
            s = {"w": None, "r": {}}
            self.state[key] = s
        return s

    def _dsem(self, key):
        k = ("d", key)
        if k not in self.sems:
            self.sems[k] = self.stack.enter_context(self.nc.semaphore("dsem%d" % len(self.sems)))
            self.dcount[k] = 0
        return k

    def _deps(self, eng, rd, wr):
        deps = []
        for key in rd:
            s = self._st(key)
            if s["w"] is not None:
                deps.append(s["w"])
            if isinstance(key, tuple) and key[0] == "ps":
                deps.extend(s["r"].values())
        for key in wr:
            s = self._st(key)
            if s["w"] is not None:
                deps.append(s["w"])
            deps.extend(s["r"].values())
        need = {}
        for (sk, val) in deps:
            if sk == eng and (eng == "pe" or not SAME_ENGINE_SYNC):
                continue
            if sk == eng and val > self.tick[eng]:
                continue
            if self.waited[eng].get(sk, 0) < val and need.get(sk, 0) < val:
                need[sk] = val
        for sk, val in need.items():
            sem = self.sems[sk]
            self.prog[eng].append(lambda b, sem=sem, val=val: b.wait_ge(sem, val))
            self.waited[eng][sk] = val
            self.nwaits += 1

    def op(self, eng, fn, rd=(), wr=(), tick=True):
        self._deps(eng, rd, wr)
        my = (eng, self.tick[eng] + 1)
        sem = self.sems[eng]
        if tick:
            self.prog[eng].append(lambda b, fn=fn, sem=sem: fn(b).then_inc(sem, 1))
            self.tick[eng] += 1
        else:
            self.prog[eng].append(lambda b, fn=fn: fn(b))
        for key in rd:
            s = self._st(key)
            old = s["r"].get(eng)
            if old is None or old[1] < my[1]:
                s["r"][eng] = my
        for key in wr:
            s = self._st(key)
            s["w"] = my
            s["r"] = {}

    def dma(self, q, out, in_, rd=(), wr=(), **kw):
        self._deps(q, rd, wr)
        key = wr[0] if wr else rd[0]
        sk = self._dsem(key)
        self.dcount[sk] += 16
        val = self.dcount[sk]
        sem = self.sems[sk]
        self.prog[q].append(lambda b, out=out, in_=in_, sem=sem, kw=kw: b.dma_start(out=out, in_=in_, **kw).then_inc(sem, 16))
        my = (sk, val)
        for key in rd:
            s = self._st(key)
            old = s["r"].get(sk)
            if old is None or old[1] < val:
                s["r"][sk] = my
        for key in wr:
            s = self._st(key)
            s["w"] = my
            s["r"] = {}

    def final_wait(self, eng, keys):
        for key in keys:
            s = self._st(key)
            deps = list(s["r"].values())
            if s["w"] is not None:
                deps.append(s["w"])
            for (sk, val) in deps:
                sem = self.sems[sk]
                self.prog[eng].append(lambda b, sem=sem, val=val: b.wait_ge(sem, val))

    def barrier(self):
        for eng in self.prog:
            for sk, sem in self.sems.items():
                if sk == eng:
                    continue
                val = self.tick[sk] if sk in self.tick else self.dcount[sk]
                if val > self.waited[eng].get(sk, 0):
                    self.prog[eng].append(lambda b, sem=sem, val=val: b.wait_ge(sem, val))
                    self.waited[eng][sk] = val
        self.state = {}

    def final_all(self, eng):
        for sk, sem in self.sems.items():
            if sk == eng:
                continue
            val = self.tick[sk] if sk in self.tick else self.dcount[sk]
            if val > 0:
                self.prog[eng].append(lambda b, sem=sem, val=val: b.wait_ge(sem, val))


    def emit(self, block):
        names = {"pe": "tensor", "act": "scalar", "dve": "vector", "pool": "gpsimd", "sp": "sync"}
        for k, attr in names.items():
            prog = self.prog[k]

            def body(b, prog=prog):
                for f in prog:
                    f(b)

            getattr(block, attr)(body)


AW = 20000 if os.environ.get("HY_DBG") else 21300


class Arena:
    def __init__(self, t, n):
        self.t, self.n, self.o = t, n, 0

    def reset(self):
        self.o = 0

    def f32(self, n):
        v = self.t[:, self.o:self.o + n]
        self.o += n
        assert self.o <= self.n, ("arena overflow", self.o)
        return v

    def bf16(self, n):
        nf = (n + 1) // 2
        v = self.t[:, self.o:self.o + nf].bitcast(BF16)
        self.o += nf
        assert self.o <= self.n, ("arena overflow", self.o)
        return v


def _pvec_layout():
    ents = []

    def add(name, n):
        ents.append((name, n))

    add("cond", 16)
    for l in range(2):
        add("b_ada%d" % l, 96)
        add("ng%d_0" % l, 16)
        add("ng%d_1" % l, 16)
    add("final_g", 16)
    add("hy_b_in", 48)
    for j in range(3):
        add("hy_cw%d" % j, 48)
    add("hy_cb", 48)
    add("hy_skip0", 16)
    add("hy_skip1", 16)
    add("hy_b_out", 16)
    for j in range(3):
        add("hy_fb%d" % j, 1)
        add("hy_ff%d" % j, 1)
    add("ret_decay", 16)
    add("flags", 8)
    off = {}
    o = 0
    for name, n in ents:
        off[name] = (o, n)
        o += n
    return off, o


PV_OFF, PV_N = _pvec_layout()
RC_N = 6 * 128 + 2
PI = float(np.pi)


def build_nc(stage=99):
    use_hy = stage in (3, 99)
    use_ret = stage in (2, 99)
    nc = bass.Bass("TRN2", target_bir_lowering=False)
    dt = nc.dram_tensor
    xin = dt("xin", [T, D], F32, kind="ExternalInput").ap()
    pvec_d = dt("pvec", [128, PV_N], F32, kind="ExternalInput").ap()
    ident_d = dt("ident", [128, 128], F32, kind="ExternalInput").ap()
    w_ada = dt("w_ada", [2, D, 6 * D], F32, kind="ExternalInput").ap()
    mlp_w1 = dt("mlp_w1", [2, D, DFF], F32, kind="ExternalInput").ap()
    mlp_w2 = dt("mlp_w2", [2, DFF, D], F32, kind="ExternalInput").ap()
    y_out = dt("y", [T, D], F32, kind="ExternalOutput").ap()
    nstate = dt("nstate", [4, 2, 8, 256, 512], F32, kind="ExternalOutput").ap()
    dbg_out = dt("dbg", [128, 8192], F32, kind="ExternalOutput").ap() if os.environ.get("HY_DBG") else None
    if use_ret:
        w_qkvg = dt("w_qkvg", [D, 6 * D], F32, kind="ExternalInput").ap()
        w_o = dt("w_o", [2 * D, D], F32, kind="ExternalInput").ap()
        state_in = dt("state_in", [2, 8, 256, 512], F32, kind="ExternalInput").ap()
        rot_d = dt("rot", [128, 2, T], F32, kind="ExternalInput").ap()
        rc_d = dt("rconst", [128, RC_N], F32, kind="ExternalInput").ap()
        gng_d = dt("gng", [128, 2 * D], F32, kind="ExternalInput").ap()
    if use_hy:
        hy_w_in = dt("hy_w_in", [D, 3 * D], F32, kind="ExternalInput").ap()
        hy_w_out = dt("hy_w_out", [D, D], F32, kind="ExternalInput").ap()
        hy_wout_f = dt("hy_f_wout", [64, 4 * D], F32, kind="ExternalInput").ap()
        hy_fw1 = dt("hy_f_w1", [33, 64], F32, kind="ExternalInput").ap()
        hy_fw2 = dt("hy_f_w2", [64, 64], F32, kind="ExternalInput").ap()
        hy_fw3 = dt("hy_f_w3", [64, 64], F32, kind="ExternalInput").ap()
        feat_d = dt("featT", [33, T], F32, kind="ExternalInput").ap()
        winf_d = dt("winf", [T, D], F32, kind="ExternalInput").ap()
        winb_d = dt("winb", [T, D], F32, kind="ExternalInput").ap()
        fwd_d = dt("fwdtab", [4, 128, 4096], BF16, kind="ExternalInput").ap()
        inv_d = dt("invtab", [4, 128, 4096], BF16, kind="ExternalInput").ap()

    from contextlib import ExitStack

    with ExitStack() as stack:
        ec = stack.enter_context
        S = Sched(nc, stack)
        sb = lambda name, shape, dtype: ec(nc.sbuf_tensor(name, shape, dtype))
        x = sb("x", [128, NDC, T], F32)
        h = sb("h", [128, NDC, T], BF16)
        slots = [sb("slot%d" % i, [128, SLOT_ELEMS], BF16) for i in range(NSLOT)]
        pv = sb("pv", [128, PV_N], F32)
        ident = sb("ident_sb", [128, 128], F32)
        identb = sb("identb_sb", [128, 128], BF16)
        ones_bf = sb("ones_bf", [128, 128], BF16)
        consts = sb("consts", [128, 4], F32)
        modT = [sb("modT%d" % l, [128, 96], F32) for l in range(2)]
        nA = sb("nA", [128, NDC], F32)
        condb = sb("condb", [128, NDC], BF16)
        arena_t = sb("arena", [128, AW], F32)
        fm1s = sb("fm1s", [64, 1024], F32) if os.environ.get("HY_DBG") else None
        A = Arena(arena_t, AW)
        ps = ec(nc.psum_tensor("ps", [128, 4096], F32))

        def bank(b, n=512):
            return ps[:, b * 512:b * 512 + n]

        def bankT(b, n=1024):
            return ps[:, b * 512:(b + 1) * 512].bitcast(BF16)[:, 0:n]

        def PB(b):
            return ("ps", b)

        def pcol(name, j=0, n=1):
            o, _ = PV_OFF[name]
            return pv[:, o + j:o + j + n]

        slot_i = [0]

        def stream_w(src, kc, ncols, q="pool", flat=False):
            i = slot_i[0] % NSLOT
            slot_i[0] += 1
            view = slots[i][:, 0:kc * ncols].rearrange("p (k n) -> p k n", n=ncols)
            if flat:
                S.dma(q, slots[i][:, 0:kc * ncols], src, wr=[("slot", i)])
            else:
                S.dma(q, view, src, wr=[("slot", i)])
            return view, ("slot", i)

        S.dma("sp", pv[:, :], pvec_d, wr=["pv"])
        S.dma("sp", ident[:, :], ident_d, wr=["ident"])
        S.dma("pool", identb[:, :], ident_d, wr=["identb"])
        S.op("dve", lambda b: b.memset(ones_bf[:, :], 1.0), wr=["ones"])
        S.op("dve", lambda b: b.memset(consts[:, 0:1], EPS), wr=["consts"])
        S.op("dve", lambda b: b.memset(consts[:, 1:2], 1.0), wr=["consts"])

        A.reset()
        tmpA = A.f32(2048)
        tmpB = A.f32(2048)
        tmps = [(tmpA, "tmpA"), (tmpB, "tmpB")]
        for tc in range(8):
            tm, tk = tmps[tc % 2]
            S.dma("sp", tm[:, :], xin[tc * 128:(tc + 1) * 128, :], wr=[tk])
            for g in range(4):
                b = (tc * 4 + g) % 8
                for j in range(4):
                    dc = g * 4 + j
                    S.op("pe", lambda e, b=b, j=j, tm=tm, dc=dc: e.transpose(
                        bank(b)[:, j * 128:(j + 1) * 128], tm[:, dc * 128:(dc + 1) * 128], ident[:, :]),
                        rd=[tk, "ident"], wr=[PB(b)], tick=(j == 3))
                outv = x[:, g * 4:(g + 1) * 4, tc * 128:(tc + 1) * 128]
                inv = bank(b).rearrange("p (j t) -> p j t", t=128)
                wrl = [("x", g * 4 + j) for j in range(4)]
                if g % 2 == 0:
                    S.op("act", lambda e, outv=outv, inv=inv: e.copy(outv, inv), rd=[PB(b)], wr=wrl)
                else:
                    S.op("dve", lambda e, outv=outv, inv=inv: e.tensor_copy(outv, inv), rd=[PB(b)], wr=wrl)

        S.op("act", lambda e: e.activation(condb[:, :], pcol("cond", 0, 16), AF.Silu), rd=["pv"], wr=["condb"])
        mod_queue = [(l, b) for l in range(2) for b in range(48)]

        mod_bank = [7]

        def mod_block(l, blk):
            mb = mod_bank[0]
            wl = w_ada[l].rearrange("(kc p) n -> p kc n", p=128)
            view, sk = stream_w(wl[:, :, blk * 256:(blk + 1) * 256], 16, 256)
            for cc in range(2):
                for kc in range(16):
                    S.op("pe", lambda e, view=view, cc=cc, kc=kc, mb=mb: e.matmul(
                        ps[:, mb * 512 + cc:mb * 512 + cc + 1], lhsT=view[:, kc, cc * 128:(cc + 1) * 128],
                        rhs=condb[:, kc:kc + 1], start=(kc == 0), stop=(kc == 15)),
                        rd=[sk, "condb"], wr=[PB(mb)], tick=(kc == 15))
            o, _ = PV_OFF["b_ada%d" % l]
            c0 = 2 * blk
            S.op("dve", lambda e, l=l, o=o, c0=c0, mb=mb: e.tensor_tensor(modT[l][:, c0:c0 + 2], ps[:, mb * 512:mb * 512 + 2],
                                                                          pv[:, o + c0:o + c0 + 2], ALU.add),
                 rd=[PB(mb), "pv"], wr=[("mod", l, c0 // 16)])

        def mod_run(n):
            for _ in range(n):
                if mod_queue:
                    l, b = mod_queue.pop(0)
                    mod_block(l, b)

        def mod_flush(l, nblk=48):
            while mod_queue and (mod_queue[0][0] < l or (mod_queue[0][0] == l and mod_queue[0][1] < nblk)):
                mod_run(1)

        if use_hy:
            mod_run(16)
        else:
            mod_run(96)

        def mod(l, which, dc):
            return modT[l][:, which * 16 + dc:which * 16 + dc + 1]

        def compute_rstd(rstd, rtmp):
            for dc in range(NDC):
                S.op("act", lambda e, dc=dc: e.activation(h[:, dc, :], x[:, dc, :], AF.Square),
                     rd=[("x", dc)], wr=[("h", dc)])
            for half in range(2):
                for dc in range(NDC):
                    S.op("pe", lambda e, dc=dc, half=half: e.matmul(
                        bank(half), lhsT=ones_bf[:, :], rhs=h[:, dc, half * 512:(half + 1) * 512],
                        start=(dc == 0), stop=(dc == NDC - 1)),
                        rd=["ones", ("h", dc)], wr=[PB(half)], tick=(dc == NDC - 1))
            S.op("act", lambda e: e.activation(rtmp, ps[:, 0:1024], AF.Sqrt, bias=consts[:, 0:1], scale=1.0 / D),
                 rd=[PB(0), PB(1), "consts"], wr=["rtmp"])
            S.op("dve", lambda e: e.reciprocal(rstd, rtmp), rd=["rtmp"], wr=["rstd"])

        def norm_mod(l, which):
            mod_flush(l, 16 if which == 0 else 48)
            S.barrier()
            A.reset()
            tA = A.f32(1024)
            tB = A.f32(1024)
            rstd = A.f32(1024)
            rtmp = A.f32(1024)
            tl = [(tA, "tmpA"), (tB, "tmpB")]
            go, _ = PV_OFF["ng%d_%d" % (l, which)]
            sc0 = (1 if which == 0 else 4) * 16
            sh0 = (0 if which == 0 else 3) * 16
            compute_rstd(rstd, rtmp)
            S.op("dve", lambda e: e.scalar_tensor_tensor(nA[:, :], modT[l][:, sc0:sc0 + 16], 1.0, pv[:, go:go + 16],
                                                         ALU.add, ALU.mult), rd=[("mod", l, sc0 // 16), "pv"], wr=["nA"])
            for dc in range(NDC):
                tm, tk = tl[dc % 2]
                S.op("dve", lambda e, dc=dc, tm=tm: e.scalar_tensor_tensor(
                    tm, x[:, dc, :], nA[:, dc:dc + 1], rstd, ALU.mult, ALU.mult),
                    rd=[("x", dc), "nA", "rstd"], wr=[tk])
                S.op("act", lambda e, dc=dc, tm=tm: e.activation(
                    h[:, dc, :], tm, AF.Identity, bias=modT[l][:, sh0 + dc:sh0 + dc + 1]),
                    rd=[tk, ("mod", l, sh0 // 16)], wr=[("h", dc)])

        def mlp(l):
            S.barrier()
            A.reset()
            hid = A.bf16(16 * T).rearrange("p (j t) -> p j t", t=T)
            tA = A.f32(512)
            tB = A.f32(512)
            tl = [(tA, "tmpA"), (tB, "tmpB")]
            w1 = mlp_w1[l].rearrange("(kc p) n -> p kc n", p=128)
            w2 = mlp_w2[l].rearrange("(kc p) n -> p kc n", p=128)
            pb = [0]
            for q in range(4):
                for blk in range(8):
                    c0 = q * 2048 + blk * 256
                    if blk % 4 == 0 and l == 0:
                        mod_run(1)
                    view, sk = stream_w(w1[:, :, c0:c0 + 256], 16, 256)
                    for cc in range(2):
                        jc = blk * 2 + cc
                        for half in range(2):
                            b = pb[0] % 6
                            pb[0] += 1
                            for kc in range(16):
                                S.op("pe", lambda e, view=view, cc=cc, kc=kc, half=half, b=b: e.matmul(
                                    bank(b), lhsT=view[:, kc, cc * 128:(cc + 1) * 128],
                                    rhs=h[:, kc, half * 512:(half + 1) * 512], start=(kc == 0), stop=(kc == 15)),
                                    rd=[sk, ("h", kc)], wr=[PB(b)], tick=(kc == 15))
                            tm, tk = tl[(pb[0]) % 2]
                            S.op("act", lambda e, b=b, tm=tm: e.activation(tm, bank(b), AF.Relu), rd=[PB(b)], wr=[tk])
                            S.op("dve", lambda e, jc=jc, half=half, tm=tm: e.tensor_tensor(
                                hid[:, jc, half * 512:(half + 1) * 512], tm, tm, ALU.mult), rd=[tk], wr=[("hid", jc)])
                for blk in range(8):
                    if blk % 4 == 0 and l == 0:
                        mod_run(1)
                    view, sk = stream_w(w2[:, q * 16:(q + 1) * 16, blk * 256:(blk + 1) * 256], 16, 256)
                    for cc in range(2):
                        dc = blk * 2 + cc
                        for half in range(2):
                            b = pb[0] % 6
                            pb[0] += 1
                            for jc in range(16):
                                S.op("pe", lambda e, view=view, cc=cc, jc=jc, half=half, b=b: e.matmul(
                                    bank(b), lhsT=view[:, jc, cc * 128:(cc + 1) * 128],
                                    rhs=hid[:, jc, half * 512:(half + 1) * 512], start=(jc == 0), stop=(jc == 15)),
                                    rd=[sk, ("hid", jc)], wr=[PB(b)], tick=(jc == 15))
                            S.op("dve", lambda e, dc=dc, half=half, b=b: e.scalar_tensor_tensor(
                                x[:, dc, half * 512:(half + 1) * 512], bank(b), mod(l, 5, dc),
                                x[:, dc, half * 512:(half + 1) * 512], ALU.mult, ALU.add),
                                rd=[PB(b), ("mod", l, 5), ("x", dc)], wr=[("x", dc)])

        def retention(l):
            mod_bank[0] = 6
            S.barrier()
            A.reset()
            qT = A.bf16(2048).rearrange("p (c t) -> p c t", t=T)
            kT = A.bf16(2048).rearrange("p (c t) -> p c t", t=T)
            ktok = A.bf16(2048).rearrange("p (n d) -> p n d", d=256)
            vv = A.bf16(4096).rearrange("p (n v) -> p n v", v=512)
            ga = A.bf16(4096).rearrange("p (n v) -> p n v", v=512)
            Sb = A.bf16(8192).rearrange("p (n c v) -> p n c v", c=2, v=512)
            Sraw = A.f32(1024).rearrange("p (c v) -> p c v", v=512)
            Sfb = A.bf16(2048).rearrange("p (k c v) -> p k c v", k=2, c=2)
            scb = A.bf16(256).rearrange("p (k i) -> p k i", i=128)
            qtl = A.bf16(1024).rearrange("p (k a i) -> p k a i", k=2, a=4)
            scrA = A.f32(512)
            scrB = A.f32(512)
            ygb = scrB.bitcast(BF16).rearrange("p (k v) -> p k v", v=512)
            ygT = A.bf16(4096).rearrange("p (c t) -> p c t", t=T)
            maskT = A.f32(128)
            mt = A.f32(256)
            qd = A.f32(256).rearrange("p (a i) -> p a i", i=128)
            gng = A.f32(512)
            rot = A.f32(2048).rearrange("p (a t) -> p a t", t=T)
            rc = A.f32(RC_N)
            lg = A.f32(16)
            kd = A.f32(16)
            cd = A.f32(16)
            cde = A.f32(16)
            st = A.f32(16)
            Dpos, Cpos, Dneg, Cneg, Ipos1, Ineg = [rc[:, i * 128:(i + 1) * 128] for i in range(6)]
            Jneg = rc[:, 768:769]
            Jpos = rc[:, 769:770]
            flag = pcol("flags", 0)

            S.dma("sp", rc, rc_d, wr=["rc"])
            S.dma("sp", rot, rot_d, wr=["rot"])
            dco, _ = PV_OFF["ret_decay"]
            S.op("act", lambda e: e.activation(lg, pv[:, dco:dco + 16], AF.Exp, scale=-1.0), rd=["pv"], wr=["lg"])
            S.op("act", lambda e: e.activation(lg, lg, AF.Ln, bias=consts[:, 1:2]), rd=["lg", "consts"], wr=["lg"])
            S.op("dve", lambda e: e.tensor_scalar(lg, lg, -1.0, None, ALU.mult), rd=["lg"], wr=["lg"])
            S.op("act", lambda e: e.activation(cd, lg, AF.Exp, scale=128.0), rd=["lg"], wr=["cd"])
            S.op("dve", lambda e: e.tensor_scalar(cde, cd, flag, None, ALU.mult), rd=["cd", "pv"], wr=["cde"])
            S.op("act", lambda e: e.activation(kd[:, 0:8], lg[:, 0:8], AF.Exp, scale=Jneg), rd=["lg", "rc"], wr=["kd"])
            S.op("act", lambda e: e.activation(kd[:, 8:16], lg[:, 8:16], AF.Exp, scale=Jpos), rd=["lg", "rc"], wr=["kd"])
            S.op("dve", lambda e: e.tensor_scalar(kd, kd, 1.0 / 16.0, None, ALU.mult), rd=["kd"], wr=["kd"])

            wq = w_qkvg.rearrange("(kc p) n -> p kc n", p=128)
            gpb = [0]

            def gbank():
                b = gpb[0] % 3
                gpb[0] += 1
                return b

            def ktok_build(hd, col):
                for tc in range(8):
                    for c in range(2):
                        S.op("pe", lambda e, tc=tc, c=c: e.transpose(
                            bankT(4)[:, c * 128:(c + 1) * 128], kT[:, c, tc * 128:(tc + 1) * 128], identb[:, :]),
                            rd=["kT", "identb"], wr=[PB(4)], tick=(c == 1))
                    S.op("act", lambda e, tc=tc: e.activation(ktok[:, tc, :], bankT(4)[:, 0:256], AF.Identity,
                                                              scale=kd[:, col:col + 1]),
                         rd=[PB(4), "kd"], wr=[("ktok", tc)])

            def state_delta(n):
                for c in range(2):
                    b = 3 if c == 0 else 7
                    S.op("pe", lambda e, n=n, c=c, b=b: e.matmul(
                        bank(b), lhsT=ktok[:, n, c * 128:(c + 1) * 128], rhs=vv[:, n, :], start=True, stop=True),
                        rd=[("ktok", n), "vv"], wr=[PB(b)])

            def state_update(scal):
                for c in range(2):
                    b = 3 if c == 0 else 7
                    S.op("dve", lambda e, c=c, b=b: e.scalar_tensor_tensor(
                        Sraw[:, c, :], Sraw[:, c, :], scal, bank(b), ALU.mult, ALU.add),
                        rd=[PB(b), "Sraw", "cd", "cde"], wr=["Sraw"])

            for hd in range(8):
                S.op("act", lambda e, hd=hd: e.activation(mt[:, 0:128], Dpos, AF.Exp, scale=lg[:, hd:hd + 1]),
                     rd=["rc", "lg"], wr=["mt"])
                S.op("act", lambda e, hd=hd: e.activation(mt[:, 128:256], Dneg, AF.Exp, scale=lg[:, 8 + hd:9 + hd]),
                     rd=["rc", "lg"], wr=["mt"])
                S.op("dve", lambda e: e.tensor_tensor(mt[:, 0:128], mt[:, 0:128], Cpos, ALU.mult), rd=["mt", "rc"], wr=["mt"])
                S.op("dve", lambda e: e.tensor_tensor(mt[:, 128:256], mt[:, 128:256], Cneg, ALU.mult), rd=["mt", "rc"], wr=["mt"])
                S.op("dve", lambda e: e.tensor_tensor(maskT, mt[:, 0:128], mt[:, 128:256], ALU.add), rd=["mt"], wr=["maskT"])
                S.op("act", lambda e, hd=hd: e.activation(qd[:, 0, :], Ipos1, AF.Exp, scale=lg[:, hd:hd + 1]),
                     rd=["rc", "lg"], wr=["qd"])
                S.op("act", lambda e, hd=hd: e.activation(qd[:, 1, :], Ineg, AF.Exp, scale=lg[:, 8 + hd:9 + hd]),
                     rd=["rc", "lg"], wr=["qd"])
                S.dma("sp", gng, gng_d[:, hd * 512:(hd + 1) * 512], wr=["gng"])

                wviews = {}

                def qk_half(qi, half, hd=hd):
                    dst, dk = ((qT, "qT"), (kT, "kT"))[qi]
                    if half == 0:
                        c0 = qi * 2048 + hd * 256
                        wviews[("qk", qi)] = stream_w(wq[:, :, c0:c0 + 256], 16, 256)
                    view, sk = wviews[("qk", qi)]
                    for cc in range(2):
                        for kc in range(16):
                            S.op("pe", lambda e, view=view, cc=cc, kc=kc, half=half: e.matmul(
                                bank(cc), lhsT=view[:, kc, cc * 128:(cc + 1) * 128],
                                rhs=h[:, kc, half * 512:(half + 1) * 512], start=(kc == 0), stop=(kc == 15)),
                                rd=[sk, ("h", kc)], wr=[PB(cc)], tick=(kc == 15))
                    hs = slice(half * 512, (half + 1) * 512)
                    cosv = rot[:, 0, hs]
                    sinv = rot[:, 1, hs]
                    S.op("dve", lambda e, cosv=cosv: e.tensor_tensor(scrA, bank(0), cosv, ALU.mult),
                         rd=[PB(0), "rot"], wr=["scrA"])
                    S.op("dve", lambda e, sinv=sinv: e.tensor_tensor(scrB, bank(1), sinv, ALU.mult),
                         rd=[PB(1), "rot"], wr=[("ygb", 0), ("ygb", 1)])
                    S.op("dve", lambda e, dst=dst, hs=hs: e.tensor_tensor(dst[:, 0, hs], scrA, scrB, ALU.subtract),
                         rd=["scrA", ("ygb", 0), ("ygb", 1)], wr=[dk])
                    S.op("dve", lambda e, sinv=sinv: e.tensor_tensor(scrA, bank(0), sinv, ALU.mult),
                         rd=[PB(0), "rot"], wr=["scrA"])
                    S.op("dve", lambda e, cosv=cosv: e.tensor_tensor(scrB, bank(1), cosv, ALU.mult),
                         rd=[PB(1), "rot"], wr=[("ygb", 0), ("ygb", 1)])
                    S.op("dve", lambda e, dst=dst, hs=hs: e.tensor_tensor(dst[:, 1, hs], scrA, scrB, ALU.add),
                         rd=["scrA", ("ygb", 0), ("ygb", 1)], wr=[dk])

                def vg_group(vi, piece, tc, hd=hd):
                    dst, dk = ((vv, "vv"), (ga, "ga"))[vi]
                    if tc == 0:
                        c0 = 4096 + vi * 4096 + hd * 512 + piece * 256
                        wviews[("vg", vi, piece)] = stream_w(wq[:, :, c0:c0 + 256], 16, 256)
                    view, sk = wviews[("vg", vi, piece)]
                    b = gbank()
                    for kc in range(16):
                        S.op("pe", lambda e, view=view, kc=kc, tc=tc, b=b: e.matmul(
                            bank(b, 256), lhsT=h[:, kc, tc * 128:(tc + 1) * 128], rhs=view[:, kc, :],
                            start=(kc == 0), stop=(kc == 15)),
                            rd=[sk, ("h", kc)], wr=[PB(b)], tick=(kc == 15))
                    fn = AF.Copy if vi == 0 else AF.Silu
                    S.op("act", lambda e, dst=dst, tc=tc, piece=piece, b=b, fn=fn: e.activation(
                        dst[:, tc, piece * 256:(piece + 1) * 256], bank(b, 256), fn),
                        rd=[PB(b)], wr=[dk])

                qk_half(1, 0)
                qk_half(1, 1)
                for piece in range(2):
                    for tc in range(8):
                        vg_group(0, piece, tc)
                extras = [lambda: qk_half(0, 0), lambda: qk_half(0, 1)]
                for piece in range(2):
                    for tc in range(8):
                        extras.append(lambda piece=piece, tc=tc: vg_group(1, piece, tc))
                for _ in range(8 if hd == 0 else (4 if hd <= 6 else 0)):
                    extras.append(lambda: mod_run(1))
                ktok_build(hd, 8 + hd)
                S.dma("sp", Sraw, state_in[1, hd].rearrange("(c p) v -> p c v", p=128), wr=["Sraw"])
                S.op("act", lambda e: e.copy(Sb[:, 7], Sraw), rd=["Sraw"], wr=[("Sb", 7)])
                for n in range(7, -1, -1):
                    state_delta(n)
                    scal = cde[:, 8 + hd:9 + hd] if n in (1, 3, 5) else cd[:, 8 + hd:9 + hd]
                    state_update(scal)
                    if n >= 1:
                        if (n - 1) in (1, 3, 5):
                            S.op("act", lambda e, n=n: e.activation(Sb[:, n - 1], Sraw, AF.Identity, scale=flag),
                                 rd=["Sraw", "pv"], wr=[("Sb", n - 1)])
                        else:
                            S.op("act", lambda e, n=n: e.copy(Sb[:, n - 1], Sraw), rd=["Sraw"], wr=[("Sb", n - 1)])
                    if n % 2 == 0:
                        S.dma("sp", nstate[n // 2, 1, hd].rearrange("(c p) v -> p c v", p=128), Sraw, rd=["Sraw"])
                    for _ in range(3 if n >= 6 else 2):
                        if extras:
                            extras.pop(0)()
                while extras:
                    extras.pop(0)()
                ktok_build(hd, hd)
                S.dma("sp", Sraw, state_in[0, hd].rearrange("(c p) v -> p c v", p=128), wr=["Sraw"])
                S.op("act", lambda e: e.copy(Sfb[:, 0], Sraw), rd=["Sraw"], wr=[("Sfb", 0)])

                def fwd_front(n, hd=hd):
                    cur = n % 2
                    ob = 6 if cur == 0 else 2
                    ns = slice(n * 128, (n + 1) * 128)
                    for c in range(2):
                        S.op("pe", lambda e, c=c, ns=ns: e.matmul(
                            bank(5, 128), lhsT=kT[:, c, ns], rhs=qT[:, c, ns], start=(c == 0), stop=(c == 1)),
                            rd=["kT", "qT"], wr=[PB(5)], tick=(c == 1))
                    S.op("dve", lambda e, cur=cur: e.tensor_tensor(scb[:, cur, :], bank(5, 128), maskT, ALU.mult),
                         rd=[PB(5), "maskT"], wr=[("scb", cur)])
                    for a in range(4):
                        S.op("dve", lambda e, a=a, cur=cur, ns=ns: e.tensor_tensor(
                            qtl[:, cur, a, :], qT[:, a % 2, ns], qd[:, a // 2, :], ALU.mult),
                            rd=["qT", "qd"], wr=[("qtl", cur)])
                    S.op("pe", lambda e, cur=cur, n=n, ob=ob: e.matmul(bank(ob), lhsT=scb[:, cur, :], rhs=vv[:, n, :],
                                                                       start=True, stop=False),
                         rd=[("scb", cur), "vv"], wr=[PB(ob)], tick=False)
                    for c in range(2):
                        S.op("pe", lambda e, cur=cur, c=c, ob=ob: e.matmul(bank(ob), lhsT=qtl[:, cur, c, :], rhs=Sfb[:, cur, c, :],
                                                                           start=False, stop=False),
                             rd=[("qtl", cur), ("Sfb", cur)], wr=[PB(ob)], tick=False)
                    for c in range(2):
                        S.op("pe", lambda e, cur=cur, c=c, n=n, ob=ob: e.matmul(bank(ob), lhsT=qtl[:, cur, 2 + c, :], rhs=Sb[:, n, c, :],
                                                                                start=False, stop=(c == 1)),
                             rd=[("qtl", cur), ("Sb", n)], wr=[PB(ob)], tick=(c == 1))
                    state_delta(n)
                    scal = cde[:, hd:hd + 1] if n in (2, 4, 6) else cd[:, hd:hd + 1]
                    state_update(scal)
                    if n < 7:
                        if (n + 1) in (2, 4, 6):
                            S.op("act", lambda e, cur=cur: e.activation(Sfb[:, 1 - cur], Sraw, AF.Identity, scale=flag),
                                 rd=["Sraw", "pv"], wr=[("Sfb", 1 - cur)])
                        else:
                            S.op("act", lambda e, cur=cur: e.copy(Sfb[:, 1 - cur], Sraw), rd=["Sraw"], wr=[("Sfb", 1 - cur)])
                    if n % 2 == 1:
                        S.dma("sp", nstate[n // 2, 0, hd].rearrange("(c p) v -> p c v", p=128), Sraw, rd=["Sraw"])

                def fwd_back(n):
                    cur = n % 2
                    ob = 6 if cur == 0 else 2
                    ns = slice(n * 128, (n + 1) * 128)
                    S.op("dve", lambda e, ob=ob: e.bn_stats(st[:, 0:6], bank(ob)), rd=[PB(ob)], wr=["st"])
                    S.op("dve", lambda e: e.bn_aggr(st[:, 6:8], st[:, 0:6]), rd=["st"], wr=["st"])
                    S.op("act", lambda e: e.activation(st[:, 8:9], st[:, 7:8], AF.Sqrt, bias=consts[:, 0:1]),
                         rd=["st", "consts"], wr=["st"])
                    S.op("dve", lambda e: e.reciprocal(st[:, 9:10], st[:, 8:9]), rd=["st"], wr=["st"])
                    S.op("dve", lambda e: e.scalar_tensor_tensor(st[:, 10:11], st[:, 6:7], -1.0, st[:, 9:10], ALU.mult, ALU.mult),
                         rd=["st"], wr=["st"])
                    S.op("act", lambda e, ob=ob: e.activation(scrA, bank(ob), AF.Identity, scale=st[:, 9:10], bias=st[:, 10:11]),
                         rd=[PB(ob), "st"], wr=["scrA"])
                    S.op("dve", lambda e: e.tensor_tensor(scrA, scrA, gng, ALU.mult), rd=["scrA", "gng"], wr=["scrA"])
                    S.op("dve", lambda e, cur=cur, n=n: e.tensor_tensor(ygb[:, cur, :], scrA, ga[:, n, :], ALU.mult),
                         rd=["scrA", "ga"], wr=[("ygb", cur)])
                    for cc in range(4):
                        S.op("pe", lambda e, cc=cc, cur=cur: e.transpose(
                            bankT(4)[:, cc * 128:(cc + 1) * 128], ygb[:, cur, cc * 128:(cc + 1) * 128], identb[:, :]),
                            rd=[("ygb", cur), "identb"], wr=[PB(4)], tick=(cc == 3))
                    S.op("act", lambda e, ns=ns: e.copy(ygT[:, :, ns], bankT(4)[:, 0:512].rearrange("p (c t) -> p c t", t=128)),
                         rd=[PB(4)], wr=["ygT"])

                for n in range(8):
                    fwd_front(n)
                    if n > 0:
                        fwd_back(n - 1)
                fwd_back(7)
                wo = w_o[hd * 512:(hd + 1) * 512, :].rearrange("(c p) n -> p c n", p=128)
                for piece in range(2):
                    view, sk = stream_w(wo[:, :, piece * 1024:(piece + 1) * 1024], 4, 1024)
                    for dcl in range(8):
                        dc = piece * 8 + dcl
                        for half in range(2):
                            b = gbank()
                            for c in range(4):
                                S.op("pe", lambda e, view=view, c=c, dcl=dcl, half=half, b=b: e.matmul(
                                    bank(b), lhsT=view[:, c, dcl * 128:(dcl + 1) * 128],
                                    rhs=ygT[:, c, half * 512:(half + 1) * 512], start=(c == 0), stop=(c == 3)),
                                    rd=[sk, "ygT"], wr=[PB(b)], tick=(c == 3))
                            S.op("dve", lambda e, dc=dc, half=half, b=b: e.scalar_tensor_tensor(
                                x[:, dc, half * 512:(half + 1) * 512], bank(b), mod(l, 2, dc),
                                x[:, dc, half * 512:(half + 1) * 512], ALU.mult, ALU.add),
                                rd=[PB(b), ("mod", l, 2), ("x", dc)], wr=[("x", dc)])

        def hyena(l):
            S.barrier()
            A.reset()
            h3b = A.bf16(1024)
            fm1 = A.f32(1024)
            fm2 = A.f32(1024)
            fm3 = A.f32(1024)
            featT = A.f32(1024)
            fw = [A.f32(64) for _ in range(3)]
            fbias = A.f32(4)
            S.dma("sp", featT[0:33, :], feat_d, wr=["featT"])
            S.dma("sp", fw[0][0:33, :], hy_fw1, wr=[("fw", 0)])
            S.dma("sp", fw[1][0:64, :], hy_fw2, wr=[("fw", 1)])
            S.dma("sp", fw[2][0:64, :], hy_fw3, wr=[("fw", 2)])
            for j in range(3):
                S.op("dve", lambda e, j=j: e.tensor_tensor(fbias[0:64, j:j + 1], pcol("hy_fb%d" % j)[0:64, :],
                                                           pcol("hy_ff%d" % j)[0:64, :], ALU.mult),
                     rd=["pv"], wr=["fbias"])
            for j in range(3):
                kk = 33 if j == 0 else 64
                src = featT if j == 0 else fm1
                for half in range(2):
                    S.op("pe", lambda e, j=j, kk=kk, src=src, half=half: e.matmul(
                        bank(half)[0:64, :], lhsT=fw[j][0:kk, 0:64], rhs=src[0:kk, half * 512:(half + 1) * 512],
                        start=True, stop=True), rd=[("fw", j), "featT", "fm1"], wr=[PB(half)])
                S.op("act", lambda e, j=j: e.activation(fm2[0:64, :], ps[0:64, 0:1024], AF.Identity,
                                                        scale=pcol("hy_ff%d" % j)[0:64, :], bias=fbias[0:64, j:j + 1]),
                     rd=[PB(0), PB(1), "pv", "fbias"], wr=["fm2"])
                S.op("dve", lambda e: e.tensor_scalar(fm3[0:64, :], fm2[0:64, :], PI, None, ALU.is_gt), rd=["fm2"], wr=["fm3"])
                S.op("dve", lambda e: e.scalar_tensor_tensor(fm2[0:64, :], fm3[0:64, :], -2.0 * PI, fm2[0:64, :], ALU.mult, ALU.add),
                     rd=["fm3", "fm2"], wr=["fm2"])
                S.op("dve", lambda e: e.tensor_scalar(fm3[0:64, :], fm2[0:64, :], -PI, None, ALU.is_lt), rd=["fm2"], wr=["fm3"])
                S.op("dve", lambda e: e.scalar_tensor_tensor(fm2[0:64, :], fm3[0:64, :], 2.0 * PI, fm2[0:64, :], ALU.mult, ALU.add),
                     rd=["fm3", "fm2"], wr=["fm2"])
                S.op("act", lambda e: e.activation(fm1[0:64, :], fm2[0:64, :], AF.Sin), rd=["fm2"], wr=["fm1"])
            S.op("act", lambda e: e.copy(h3b[0:64, :], fm1[0:64, :]), rd=["fm1"], wr=["h3b"])
            if dbg_out is not None:
                S.op("act", lambda e: e.copy(fm1s[0:64, :], fm1[0:64, :]), rd=["fm1"], wr=["fm1s"])

            HY_CUT = int(os.environ.get('HY_CUT', '99'))
            if HY_CUT <= 1:
                return
            S.barrier()
            A.reset()
            h3b = A.bf16(1024)
            Pb = [A.f32(1024) for _ in range(2)]
            ACC0 = A.f32(1024)
            ACCb = [ACC0, ACC0]
            vcb = [A.bf16(1024) for _ in range(2)]
            x1cb = [A.bf16(1024) for _ in range(2)]
            x2cb = [A.bf16(1024) for _ in range(2)]
            Rre = A.bf16(2048).rearrange("p (t c) -> p t c", c=256)
            Rim = A.bf16(2048).rearrange("p (t c) -> p t c", c=256)
            SPre = A.f32(2048).rearrange("p (f c) -> p f c", c=256)
            SPim = A.f32(2048).rearrange("p (f c) -> p f c", c=256)
            Y = A.bf16(2048).rearrange("p (f c) -> p f c", c=128)
            z1 = A.bf16(1024)
            z2g = A.bf16(4096).rearrange("p (b t) -> p b t", t=T)
            t1 = A.f32(1024)
            t2 = A.f32(1024)
            t1v = t1.rearrange("p (t c) -> p t c", c=128)
            t2v = t2.rearrange("p (t c) -> p t c", c=128)
            winf = A.f32(1024).rearrange("p (t c) -> p t c", c=128)
            winb = A.f32(1024).rearrange("p (t c) -> p t c", c=128)
            woutb = A.bf16(512).rearrange("p (g c) -> p g c", c=128)
            w0n = A.f32(48)
            w2n = A.f32(48)
            gb = A.f32(16)
            fm1o = pcol("flags", 1)
            o0, _ = PV_OFF["hy_cw0"]
            o2, _ = PV_OFF["hy_cw2"]
            S.op("dve", lambda e: e.tensor_scalar(w0n, pv[:, o0:o0 + 48], fm1o, None, ALU.mult), rd=["pv"], wr=["w0n"])
            S.op("dve", lambda e: e.tensor_scalar(w2n, pv[:, o2:o2 + 48], fm1o, None, ALU.mult), rd=["pv"], wr=["w2n"])

            win = hy_w_in.rearrange("(kc p) n -> p kc n", p=128)
            wff = hy_wout_f.rearrange("k (g c) -> k g c", c=D)
            winf_v = winf_d.rearrange("(t p) c -> p t c", p=128)
            winb_v = winb_d.rearrange("(t p) c -> p t c", p=128)
            opb = [0]
            pacc = [0]

            def transpose_to_R(src, sk):
                for tc in range(8):
                    S.op("pe", lambda e, tc=tc: e.transpose(
                        bankT(2)[:, tc * 128:(tc + 1) * 128], src[:, tc * 128:(tc + 1) * 128], identb[:, :]),
                        rd=[sk, "identb"], wr=[PB(2)], tick=(tc == 7))
                pv3 = bankT(2)[:, 0:1024].rearrange("p (t c) -> p t c", c=128)
                S.op("act", lambda e: e.copy(Rre[:, :, 128:256], pv3), rd=[PB(2)], wr=["RreU"])
                S.op("dve", lambda e: e.tensor_copy(Rim[:, :, 128:256], Rre[:, :, 128:256]), rd=["RreU"], wr=["RimU"])

            def stageA(blk, tis=(0, 1, 2)):
                s = blk % 2
                for ti, (dst, dk) in enumerate(((vcb[s], ("vc", s)), (x1cb[s], ("x1c", s)), (x2cb[s], ("x2c", s)))):
                    if ti not in tis:
                        continue
                    c0 = ti * 2048 + blk * 128
                    col = ti * 16 + blk
                    pi = pacc[0] % 2
                    pacc[0] += 1
                    P, ACC, pk, ak = Pb[pi], ACCb[pi], ("P", pi), "ACC"
                    view, sk = stream_w(win[:, :, c0:c0 + 128], 16, 128)
                    for half in range(2):
                        for kc in range(16):
                            S.op("pe", lambda e, view=view, kc=kc, half=half: e.matmul(
                                bank(half), lhsT=view[:, kc, :], rhs=h[:, kc, half * 512:(half + 1) * 512],
                                start=(kc == 0), stop=(kc == 15)),
                                rd=[sk, ("h", kc)], wr=[PB(half)], tick=(kc == 15))
                    S.op("act", lambda e, col=col, P=P: e.activation(P, ps[:, 0:1024], AF.Identity, bias=pcol("hy_b_in", col)),
                         rd=[PB(0), PB(1), "pv"], wr=[pk])
                    S.op("act", lambda e, col=col, P=P, ACC=ACC: e.activation(ACC, P, AF.Identity, scale=pcol("hy_cw1", col),
                                                                              bias=pcol("hy_cb", col)), rd=[pk, "pv"], wr=[ak])
                    S.op("dve", lambda e, col=col, P=P, ACC=ACC: e.scalar_tensor_tensor(
                        ACC[:, 1:T], P[:, 0:T - 1], pcol("hy_cw0", col), ACC[:, 1:T], ALU.mult, ALU.add),
                        rd=[pk, ak, "pv"], wr=[ak])
                    S.op("dve", lambda e, col=col, P=P, ACC=ACC: e.scalar_tensor_tensor(
                        ACC[:, 0:T - 1], P[:, 1:T], pcol("hy_cw2", col), ACC[:, 0:T - 1], ALU.mult, ALU.add),
                        rd=[pk, ak, "pv"], wr=[ak])
                    S.op("dve", lambda e, col=col, P=P, ACC=ACC: e.scalar_tensor_tensor(
                        ACC[:, 256:1024:256], P[:, 255:1023:256], w0n[:, col:col + 1], ACC[:, 256:1024:256], ALU.mult, ALU.add),
                        rd=[pk, ak, "w0n"], wr=[ak])
                    S.op("dve", lambda e, col=col, P=P, ACC=ACC: e.scalar_tensor_tensor(
                        ACC[:, 255:1023:256], P[:, 256:1024:256], w2n[:, col:col + 1], ACC[:, 255:1023:256], ALU.mult, ALU.add),
                        rd=[pk, ak, "w2n"], wr=[ak])
                    S.op("act", lambda e, dst=dst, ACC=ACC: e.copy(dst, ACC), rd=[ak], wr=[dk])

            def out_proj_group(g4):
                wo = hy_w_out[g4 * 512:(g4 + 1) * 512, :].rearrange("(c p) n -> p c n", p=128)
                for piece in range(2):
                    view, sk = stream_w(wo[:, :, piece * 1024:(piece + 1) * 1024], 4, 1024)
                    for dcl in range(8):
                        dc = piece * 8 + dcl
                        for half in range(2):
                            b = opb[0] % 2
                            opb[0] += 1
                            for c in range(4):
                                S.op("pe", lambda e, view=view, c=c, dcl=dcl, half=half, b=b: e.matmul(
                                    bank(b), lhsT=view[:, c, dcl * 128:(dcl + 1) * 128],
                                    rhs=z2g[:, c, half * 512:(half + 1) * 512], start=(c == 0), stop=(c == 3)),
                                    rd=[sk, ("z2g", c)], wr=[PB(b)], tick=(c == 3))
                            S.op("dve", lambda e, dc=dc, half=half, b=b: e.scalar_tensor_tensor(
                                x[:, dc, half * 512:(half + 1) * 512], bank(b), mod(l, 2, dc),
                                x[:, dc, half * 512:(half + 1) * 512], ALU.mult, ALU.add),
                                rd=[PB(b), ("mod", l, 2), ("x", dc)], wr=[("x", dc)])

            stageA(0)
            for blk in range(16):
                gi = blk % 4
                s = blk % 2
                vc, x1c, x2c = vcb[s], x1cb[s], x2cb[s]
                vk, x1k, x2k = ("vc", s), ("x1c", s), ("x2c", s)
                cs = slice(blk * 128, (blk + 1) * 128)
                transpose_to_R(vc, vk)
                S.dma("pool", woutb[0:64, :, :], wff[:, :, cs], wr=["woutb"])
                S.dma("sp", winf, winf_v[:, :, cs], wr=["winf"])
                S.dma("sp", winb, winb_v[:, :, cs], wr=["winb"])
                for o in range(2):
                    for tc in range(8):
                        S.op("pe", lambda e, tc=tc, o=o: e.matmul(
                            bank(4 + tc // 2)[:, (tc % 2) * 256:(tc % 2) * 256 + 256], lhsT=h3b[0:64, tc * 128:(tc + 1) * 128],
                            rhs=woutb[0:64, 2 * o:2 * o + 2, :], start=True, stop=True),
                            rd=["h3b", "woutb"], wr=[PB(4 + tc // 2)], tick=(tc % 2 == 1))
                    F = ps[:, 2048:4096].rearrange("p (t d c) -> p t d c", d=2, c=128)
                    S.op("dve", lambda e, F=F: e.tensor_tensor(t1v, F[:, :, 0, :], winf, ALU.mult),
                         rd=[PB(4), PB(5), PB(6), PB(7), "winf"], wr=["t1"])
                    S.op("dve", lambda e, F=F: e.tensor_tensor(t2v, F[:, :, 1, :], winb, ALU.mult),
                         rd=[PB(4), PB(5), PB(6), PB(7), "winb"], wr=["t2"])
                    S.op("dve", lambda e: e.tensor_tensor(Rre[:, :, 0:128], t1v, t2v, ALU.add), rd=["t1", "t2"], wr=["RreK"])
                    S.op("dve", lambda e: e.tensor_tensor(Rim[:, :, 0:128], t1v, t2v, ALU.subtract), rd=["t1", "t2"], wr=["RimK"])
                    for part in range(2):
                        R = Rre if part == 0 else Rim
                        rk = ["RreK", "RreU"] if part == 0 else ["RimK", "RimU"]
                        for pc in range(2):
                            view, sk = stream_w(fwd_d[part * 2 + pc], 8, 512, q="sp", flat=True)
                            for fl in range(4):
                                fc = pc * 4 + fl
                                reg = bank(4 + fc // 2)[:, (fc % 2) * 256:(fc % 2) * 256 + 256]
                                for tc in range(8):
                                    S.op("pe", lambda e, view=view, fl=fl, tc=tc, reg=reg, R=R: e.matmul(
                                        reg, lhsT=view[:, tc, fl * 128:(fl + 1) * 128], rhs=R[:, tc, :],
                                        start=(tc == 0), stop=(tc == 7)),
                                        rd=[sk] + rk, wr=[PB(4 + fc // 2)], tick=(tc == 7))
                        src = ps[:, 2048:4096].rearrange("p (f c) -> p f c", c=256)
                        if part == 0:
                            S.op("act", lambda e, src=src: e.copy(SPre, src), rd=[PB(4), PB(5), PB(6), PB(7)], wr=["SPre"])
                        else:
                            S.op("act", lambda e, src=src: e.copy(SPim, src), rd=[PB(4), PB(5), PB(6), PB(7)], wr=["SPim"])
                    Kre, Ure = SPre[:, :, 0:128], SPre[:, :, 128:256]
                    Kim, Uim = SPim[:, :, 0:128], SPim[:, :, 128:256]
                    S.op("dve", lambda e: e.tensor_tensor(t1v, Ure, Kre, ALU.mult), rd=["SPre"], wr=["t1"])
                    S.op("dve", lambda e: e.tensor_tensor(t2v, Uim, Kim, ALU.mult), rd=["SPim"], wr=["t2"])
                    S.op("dve", lambda e: e.tensor_tensor(Y[:, 0:8, :], t1v, t2v, ALU.subtract), rd=["t1", "t2"], wr=["Yre"])
                    S.op("dve", lambda e: e.tensor_tensor(t1v, Ure, Kim, ALU.mult), rd=["SPre", "SPim"], wr=["t1"])
                    S.op("dve", lambda e: e.tensor_tensor(t2v, Uim, Kre, ALU.mult), rd=["SPre", "SPim"], wr=["t2"])
                    S.op("dve", lambda e: e.tensor_tensor(Y[:, 8:16, :], t1v, t2v, ALU.add), rd=["t1", "t2"], wr=["Yim"])
                    if blk + 1 < 16:
                        stageA(blk + 1, (0, 1) if o == 0 else (2,))
                    mod_run(1)
                    if o == 1 and gi == 0 and blk > 0:
                        out_proj_group(blk // 4 - 1)
                    for pc in range(4):
                        view, sk = stream_w(inv_d[pc], 16, 256, q="sp", flat=True)
                        reg = bank(2 + pc // 2)[:, (pc % 2) * 256:(pc % 2) * 256 + 256]
                        for fc in range(16):
                            S.op("pe", lambda e, view=view, fc=fc, reg=reg: e.matmul(
                                reg, lhsT=Y[:, fc, :], rhs=view[:, fc, :], start=(fc == 0), stop=(fc == 15)),
                                rd=[sk, "Yre", "Yim"], wr=[PB(2 + pc // 2)], tick=(fc == 15))
                    yps = ps[:, 1024:2048]
                    if o == 0:
                        S.op("act", lambda e: e.copy(t2, yps), rd=[PB(2), PB(3)], wr=["t2"])
                        S.op("act", lambda e, blk=blk, vc=vc: e.activation(t1, vc, AF.Identity, scale=pcol("hy_skip0", blk)),
                             rd=[vk, "pv"], wr=["t1"])
                        S.op("dve", lambda e: e.tensor_tensor(t1, t1, t2, ALU.add), rd=["t1", "t2"], wr=["t1"])
                        S.op("dve", lambda e, x1c=x1c: e.tensor_tensor(z1, t1, x1c, ALU.mult), rd=["t1", x1k], wr=["z1"])
                        transpose_to_R(z1, "z1")
                    else:
                        S.op("act", lambda e: e.copy(t2, yps), rd=[PB(2), PB(3)], wr=["t2"])
                        S.op("act", lambda e, blk=blk: e.activation(t1, z1, AF.Identity, scale=pcol("hy_skip1", blk)),
                             rd=["z1", "pv"], wr=["t1"])
                        S.op("dve", lambda e: e.tensor_tensor(t1, t1, t2, ALU.add), rd=["t1", "t2"], wr=["t1"])
                        S.op("dve", lambda e, gi=gi, x2c=x2c: e.tensor_tensor(z2g[:, gi, :], t1, x2c, ALU.mult),
                             rd=["t1", x2k], wr=[("z2g", gi)])
            out_proj_group(3)
            bo, _ = PV_OFF["hy_b_out"]
            S.op("dve", lambda e: e.tensor_tensor(gb, modT[l][:, 32:48], pv[:, bo:bo + 16], ALU.mult),
                 rd=[("mod", l, 2), "pv"], wr=["gb"])
            for dc in range(NDC):
                S.op("act", lambda e, dc=dc: e.activation(x[:, dc, :], x[:, dc, :], AF.Identity, bias=gb[:, dc:dc + 1]),
                     rd=[("x", dc), "gb"], wr=[("x", dc)])

        if use_hy:
            norm_mod(0, 0)
            hyena(0)
        norm_mod(0, 1)
        mlp(0)
        if use_ret:
            norm_mod(1, 0)
            retention(1)
            mod_bank[0] = 7
        norm_mod(1, 1)
        mlp(1)

        S.barrier()
        A.reset()
        tmpA = A.f32(2048)
        tmpB = A.f32(2048)
        rstd = A.f32(1024)
        rtmp = A.f32(1024)
        tmps = [(tmpA, "tmpA"), (tmpB, "tmpB")]
        compute_rstd(rstd, rtmp)
        fo, _ = PV_OFF["final_g"]
        for dc in range(NDC):
            S.op("dve", lambda e, dc=dc: e.scalar_tensor_tensor(
                x[:, dc, :], x[:, dc, :], pv[:, fo + dc:fo + dc + 1], rstd, ALU.mult, ALU.mult),
                rd=[("x", dc), "pv", "rstd"], wr=[("x", dc)])
        for tc in range(8):
            tm, tk = tmps[tc % 2]
            for g in range(4):
                b = (tc * 4 + g) % 8
                for j in range(4):
                    dc = g * 4 + j
                    S.op("pe", lambda e, b=b, j=j, dc=dc, tc=tc: e.transpose(
                        bank(b)[:, j * 128:(j + 1) * 128], x[:, dc, tc * 128:(tc + 1) * 128], ident[:, :]),
                        rd=[("x", dc), "ident"], wr=[PB(b)], tick=(j == 3))
                if g % 2 == 0:
                    S.op("act", lambda e, b=b, g=g, tm=tm: e.copy(tm[:, g * 512:(g + 1) * 512], bank(b)),
                         rd=[PB(b)], wr=[tk])
                else:
                    S.op("dve", lambda e, b=b, g=g, tm=tm: e.tensor_copy(tm[:, g * 512:(g + 1) * 512], bank(b)),
                         rd=[PB(b)], wr=[tk])
            S.dma("sp", y_out[tc * 128:(tc + 1) * 128, :], tm[:, :], rd=[tk])
        S.barrier()
        S.final_all("sp")

        with nc.Block() as block:
            S.emit(block)
    return nc


PERM = np.concatenate([np.arange(0, 256, 2), np.arange(1, 256, 2)])
MIN_DECAY = float(np.log(1e-2) / 1.5)
MAX_DECAY = float(np.log(1e-2) / 0.3)


def _core_consts(sample):
    L = 1024 if sample else 256
    nseq = T // L
    f32 = np.float32
    pos = np.arange(L, dtype=f32)
    t = (pos / f32(L)).astype(f32)
    omega = (f32(2.0 * np.pi) * pos / f32(L)).astype(f32)
    bands = np.linspace(1e-4, 15, 16, dtype=f32)
    ang = (omega[:, None] * bands[None, :]).astype(f32)
    feat = np.concatenate([t[:, None], np.cos(ang), -np.sin(ang)], axis=-1).astype(f32)
    featT = np.tile(feat.T, (1, nseq)).astype(f32)
    deltas = np.abs(np.linspace(MIN_DECAY, MAX_DECAY, D, dtype=f32))
    window = np.exp(-t[:, None] * deltas[None, :]).astype(f32)
    winb = window.copy()
    winb[0, :] = 0.0
    winf_t = np.tile(window, (nseq, 1))
    winb_t = np.tile(winb, (nseq, 1))
    n = 2 * L
    fi = np.arange(L, dtype=np.float64)
    ti = np.arange(L, dtype=np.float64)
    a = 2.0 * np.pi * (fi[None, :] + 0.5) * ti[:, None] / n
    C = np.cos(a)
    Sn = np.sin(a)
    fwd = np.zeros((T, 2 * T), np.float64)
    inv = np.zeros((2 * T, T), np.float64)
    for s in range(nseq):
        sl = slice(s * L, (s + 1) * L)
        fwd[sl, s * L:(s + 1) * L] = C
        fwd[sl, T + s * L:T + (s + 1) * L] = -Sn
        inv[s * L:(s + 1) * L, sl] = (2.0 / n) * C.T
        inv[T + s * L:T + (s + 1) * L, sl] = -(2.0 / n) * Sn.T
    rot = np.zeros((128, 2, T), f32)
    if sample:
        rows = np.repeat(np.arange(16, dtype=f32), 64)
        cols = np.tile(np.arange(64, dtype=f32), 16)
        invf = (f32(10000.0) ** (-np.arange(64, dtype=f32) / f32(64))).astype(f32)
        ang2 = np.concatenate([rows[:, None] * invf, cols[:, None] * invf], axis=-1).astype(f32)
        rot[:, 0, :] = np.cos(ang2).T
        rot[:, 1, :] = np.sin(ang2).T
    else:
        rot[:, 0, :] = 1.0
    j = np.arange(128, dtype=f32)[:, None]
    i = np.arange(128, dtype=f32)[None, :]
    rc = np.zeros((128, RC_N), f32)
    rc[:, 0:128] = np.maximum(i - j, 0)
    rc[:, 128:256] = (i >= j).astype(f32) / 16.0
    rc[:, 256:384] = np.maximum(j - i, 0)
    rc[:, 384:512] = (j >= i).astype(f32) / 16.0
    rc[:, 512:640] = np.broadcast_to(i + 1.0, (128, 128))
    rc[:, 640:768] = np.broadcast_to(128.0 - i, (128, 128))
    rc[:, 768] = 127.0 - j[:, 0]
    rc[:, 769] = j[:, 0]
    return dict(featT=featT, winf=winf_t.astype(f32), winb=winb_t.astype(f32),
                fwdtab=np.ascontiguousarray(fwd.reshape(8, 128, 4, 512).transpose(2, 1, 0, 3).reshape(4, 128, 4096)).astype(ml_dtypes.bfloat16), invtab=np.ascontiguousarray(inv.reshape(16, 128, 4, 256).transpose(2, 1, 0, 3).reshape(4, 128, 4096)).astype(ml_dtypes.bfloat16), rot=rot, rconst=rc)


def _pvec_fill(vals, bvals):
    pv = np.zeros((128, PV_N), np.float32)
    for name, v in vals.items():
        o, n = PV_OFF[name]
        v = np.asarray(v, np.float32).reshape(-1)
        if v.size < n * 128:
            v = np.concatenate([v, np.zeros(n * 128 - v.size, np.float32)])
        pv[:, o:o + n] = v.reshape(n, 128).T
    for name, v in bvals.items():
        o, n = PV_OFF[name]
        v = np.asarray(v, np.float32).reshape(-1)
        pv[:, o:o + v.size] = v[None, :]
    return pv


def make_in_maps(inputs, cores, stage=99):
    use_hy = stage in (3, 99)
    use_ret = stage in (2, 99)
    g = {k: np.asarray(v) for k, v in inputs.items()}
    ident = np.eye(128, dtype=np.float32)
    cc = {True: _core_consts(True), False: _core_consts(False)}
    shared = {"ident": ident, "w_ada": g["w_ada"], "mlp_w1": g["mlp_w1"], "mlp_w2": g["mlp_w2"]}
    if use_ret:
        wq = np.array(g["ret_w_qkvg"][0], dtype=np.float32, copy=True)
        for qi in range(2):
            for hd in range(8):
                c0 = qi * 2048 + hd * 256
                wq[:, c0:c0 + 256] = g["ret_w_qkvg"][0][:, c0 + PERM]
        shared["w_qkvg"] = wq
        shared["w_o"] = np.ascontiguousarray(g["ret_w_o"][0])
        shared["gng"] = np.ascontiguousarray(np.broadcast_to(g["ret_gn_g"][0][None, :], (128, 2 * D))).astype(np.float32)
    if use_hy:
        shared["hy_w_in"] = np.ascontiguousarray(g["hy_w_in"][0])
        shared["hy_w_out"] = np.ascontiguousarray(g["hy_w_out"][0])
        shared["hy_f_wout"] = np.ascontiguousarray(g["hy_f_wout"][0])
        shared["hy_f_w1"] = np.ascontiguousarray(g["hy_f_w1"][0])
        shared["hy_f_w2"] = np.ascontiguousarray(g["hy_f_w2"][0])
        shared["hy_f_w3"] = np.ascontiguousarray(g["hy_f_w3"][0])
    maps = []
    for c in cores:
        sample = c < 4
        if sample:
            xin = g["x_sample"][c]
            cond = g["c"][c]
        else:
            i = c - 4
            xin = g["x_prompt"][4 * i:4 * i + 4].reshape(T, D)
            cond = g["c_ctx"]
        vals = {"cond": cond, "final_g": g["final_g"],
                "hy_b_in": g["hy_b_in"][0], "hy_cb": g["hy_conv_b"][0],
                "hy_skip0": g["hy_f_skip"][0, 0], "hy_skip1": g["hy_f_skip"][0, 1], "hy_b_out": g["hy_b_out"][0]}
        for j in range(3):
            vals["hy_cw%d" % j] = g["hy_conv_w"][0, j]
            vals["hy_fb%d" % j] = (g["hy_f_b1"], g["hy_f_b2"], g["hy_f_b3"])[j][0]
            vals["hy_ff%d" % j] = g["hy_f_freq"][0, j]
        for l in range(2):
            vals["b_ada%d" % l] = g["b_ada"][l]
            vals["ng%d_0" % l] = g["norm_g"][l, 0]
            vals["ng%d_1" % l] = g["norm_g"][l, 1]
        fl = 1.0 if sample else 0.0
        bvals = {"ret_decay": g["ret_decay"][0].reshape(-1), "flags": np.array([fl, fl - 1.0], np.float32)}
        m = dict(shared)
        m["xin"] = np.ascontiguousarray(xin, dtype=np.float32)
        m["pvec"] = _pvec_fill(vals, bvals)
        k = cc[sample]
        if use_ret:
            if sample:
                st = np.ascontiguousarray(g["state_ret"][c, 0][:, :, PERM, :]).astype(np.float32)
            else:
                st = np.zeros((2, 8, 256, 512), np.float32)
            m["state_in"] = st
            m["rot"] = k["rot"]
            m["rconst"] = k["rconst"]
        if use_hy:
            for nm in ("featT", "winf", "winb", "fwdtab", "invtab"):
                m[nm] = k[nm]
        maps.append(m)
    return maps


_NC_CACHE = {}


def kernel(**inputs):
    if "nc" not in _NC_CACHE:
        _NC_CACHE["nc"] = build_nc()
    nc = _NC_CACHE["nc"]
    cores = list(range(8))
    in_maps = make_in_maps(inputs, cores)
    res = run_bass_kernel_spmd(nc, in_maps, core_ids=cores)
    ys = [np.asarray(r["y"]) for r in res.results]
    y_sample = np.stack(ys[0:4], axis=0).astype(np.float32)
    y_prompt = np.concatenate([ys[4 + i].reshape(4, 256, D) for i in range(4)], axis=0).astype(np.float32)
    new_state = np.zeros((16, 1, 2, 8, 256, 512), np.float32)
    for i in range(4):
        ns = np.asarray(res.results[4 + i]["nstate"]).reshape(4, 2, 8, 256, 512)
        new_state[4 * i:4 * i + 4, 0][:, :, :, PERM, :] = ns
    return (y_prompt, y_sample, new_state)
```

```python
import os
import numpy as np
import ml_dtypes
import concourse.bass as bass
import concourse.mybir as mybir
from concourse.bass_utils import run_bass_kernel_spmd

F32 = mybir.dt.float32
BF16 = mybir.dt.bfloat16
AF = mybir.ActivationFunctionType
ALU = mybir.AluOpType

D = 2048
T = 1024
NDC = 16
DFF = 8192
EPS = 1e-6
NSLOT = 3
SLOT_ELEMS = 4096
SAME_ENGINE_SYNC = True


class Sched:
    def __init__(self, nc, stack):
        self.nc = nc
        self.stack = stack
        self.prog = {k: [] for k in ("pe", "act", "dve", "pool", "sp")}
        self.tick = {k: 0 for k in self.prog}
        self.waited = {k: {} for k in self.prog}
        self.sems = {}
        for k in self.prog:
            self.sems[k] = stack.enter_context(nc.semaphore("sem_" + k))
        self.state = {}
        self.dcount = {}
        self.nwaits = 0

    def _st(self, key):
        s = self.state.get(key)
        if s is None:
            s = {"w": None, "r": {}}
            self.state[key] = s
        return s

    def _dsem(self, key):
        k = ("d", key)
        if k not in self.sems:
            self.sems[k] = self.stack.enter_context(self.nc.semaphore("dsem%d" % len(self.sems)))
            self.dcount[k] = 0
        return k

    def _deps(self, eng, rd, wr):
        deps = []
        for key in rd:
            s = self._st(key)
            if s["w"] is not None:
                deps.append(s["w"])
            if isinstance(key, tuple) and key[0] == "ps":
                deps.extend(s["r"].values())
        for key in wr:
            s = self._st(key)
            if s["w"] is not None:
                deps.append(s["w"])
            deps.extend(s["r"].values())
        need = {}
        for (sk, val) in deps:
            if sk == eng and (eng == "pe" or not SAME_ENGINE_SYNC):
                continue
            if sk == eng and val > self.tick[eng]:
                continue
            if self.waited[eng].get(sk, 0) < val and need.get(sk, 0) < val:
                need[sk] = val
        for sk, val in need.items():
            sem = self.sems[sk]
            self.prog[eng].append(lambda b, sem=sem, val=val: b.wait_ge(sem, val))
            self.waited[eng][sk] = val
            self.nwaits += 1

    def op(self, eng, fn, rd=(), wr=(), tick=True):
        self._deps(eng, rd, wr)
        my = (eng, self.tick[eng] + 1)
        sem = self.sems[eng]
        if tick:
            self.prog[eng].append(lambda b, fn=fn, sem=sem: fn(b).then_inc(sem, 1))
            self.tick[eng] += 1
        else:
            self.prog[eng].append(lambda b, fn=fn: fn(b))
        for key in rd:
            s = self._st(key)
            old = s["r"].get(eng)
            if old is None or old[1] < my[1]:
                s["r"][eng] = my
        for key in wr:
            s = self._st(key)
            s["w"] = my
            s["r"] = {}

    def dma(self, q, out, in_, rd=(), wr=(), **kw):
        self._deps(q, rd, wr)
        key = wr[0] if wr else rd[0]
        sk = self._dsem(key)
        self.dcount[sk] += 16
        val = self.dcount[sk]
        sem = self.sems[sk]
        self.prog[q].append(lambda b, out=out, in_=in_, sem=sem, kw=kw: b.dma_start(out=out, in_=in_, **kw).then_inc(sem, 16))
        my = (sk, val)
        for key in rd:
            s = self._st(key)
            old = s["r"].get(sk)
            if old is None or old[1] < val:
                s["r"][sk] = my
        for key in wr:
            s = self._st(key)
            s["w"] = my
            s["r"] = {}

    def final_wait(self, eng, keys):
        for key in keys:
            s = self._st(key)
            deps = list(s["r"].values())
            if s["w"] is not None:
                deps.append(s["w"])
            for (sk, val) in deps:
                sem = self.sems[sk]
                self.prog[eng].append(lambda b, sem=sem, val=val: b.wait_ge(sem, val))

    def barrier(self):
        for eng in self.prog:
            for sk, sem in self.sems.items():
                if sk == eng:
                    continue
                val = self.tick[sk] if sk in self.tick else self.dcount[sk]
                if val > self.waited[eng].get(sk, 0):
                    self.prog[eng].append(lambda b, sem=sem, val=val: b.wait_ge(sem, val))
                    self.waited[eng][sk] = val
        self.state = {}

    def final_all(self, eng):
        for sk, sem in self.sems.items():
            if sk == eng:
                continue
            val = self.tick[sk] if sk in self.tick else self.dcount[sk]
            if val > 0:
                self.prog[eng].append(lambda b, sem=sem, val=val: b.wait_ge(sem, val))


    def emit(self, block):
        names = {"pe": "tensor", "act": "scalar", "dve": "vector", "pool": "gpsimd", "sp": "sync"}
        for k, attr in names.items():
            prog = self.prog[k]

            def body(b, prog=prog):
                for f in prog:
                    f(b)

            getattr(block, attr)(body)


AW = 20000 if os.environ.get("HY_DBG") else 21300


class Arena:
    def __init__(self, t, n):
        self.t, self.n, self.o = t, n, 0

    def reset(self):
        self.o = 0

    def f32(self, n):
        v = self.t[:, self.o:self.o + n]
        self.o += n
        assert self.o <= self.n, ("arena overflow", self.o)
        return v

    def bf16(self, n):
        nf = (n + 1) // 2
        v = self.t[:, self.o:self.o + nf].bitcast(BF16)
        self.o += nf
        assert self.o <= self.n, ("arena overflow", self.o)
        return v


def _pvec_layout():
    ents = []

    def add(name, n):
        ents.append((name, n))

    add("cond", 16)
    for l in range(2):
        add("b_ada%d" % l, 96)
        add("ng%d_0" % l, 16)
        add("ng%d_1" % l, 16)
    add("final_g", 16)
    add("hy_b_in", 48)
    for j in range(3):
        add("hy_cw%d" % j, 48)
    add("hy_cb", 48)
    add("hy_skip0", 16)
    add("hy_skip1", 16)
    add("hy_b_out", 16)
    for j in range(3):
        add("hy_fb%d" % j, 1)
        add("hy_ff%d" % j, 1)
    add("ret_decay", 16)
    add("flags", 8)
    off = {}
    o = 0
    for name, n in ents:
        off[name] = (o, n)
        o += n
    return off, o


PV_OFF, PV_N = _pvec_layout()
RC_N = 6 * 128 + 2
PI = float(np.pi)


def build_nc(stage=99):
    use_hy = stage in (3, 99)
    use_ret = stage in (2, 99)
    nc = bass.Bass("TRN2", target_bir_lowering=False)
    dt = nc.dram_tensor
    xin = dt("xin", [T, D], F32, kind="ExternalInput").ap()
    pvec_d = dt("pvec", [128, PV_N], F32, kind="ExternalInput").ap()
    ident_d = dt("ident", [128, 128], F32, kind="ExternalInput").ap()
    w_ada = dt("w_ada", [2, D, 6 * D], F32, kind="ExternalInput").ap()
    mlp_w1 = dt("mlp_w1", [2, D, DFF], F32, kind="ExternalInput").ap()
    mlp_w2 = dt("mlp_w2", [2, DFF, D], F32, kind="ExternalInput").ap()
    y_out = dt("y", [T, D], F32, kind="ExternalOutput").ap()
    nstate = dt("nstate", [4, 2, 8, 256, 512], F32, kind="ExternalOutput").ap()
    dbg_out = dt("dbg", [128, 8192], F32, kind="ExternalOutput").ap() if os.environ.get("HY_DBG") else None
    if use_ret:
        w_qkvg = dt("w_qkvg", [D, 6 * D], F32, kind="ExternalInput").ap()
        w_o = dt("w_o", [2 * D, D], F32, kind="ExternalInput").ap()
        state_in = dt("state_in", [2, 8, 256, 512], F32, kind="ExternalInput").ap()
        rot_d = dt("rot", [128, 2, T], F32, kind="ExternalInput").ap()
        rc_d = dt("rconst", [128, RC_N], F32, kind="ExternalInput").ap()
        gng_d = dt("gng", [128, 2 * D], F32, kind="ExternalInput").ap()
    if use_hy:
        hy_w_in = dt("hy_w_in", [D, 3 * D], F32, kind="ExternalInput").ap()
        hy_w_out = dt("hy_w_out", [D, D], F32, kind="ExternalInput").ap()
        hy_wout_f = dt("hy_f_wout", [64, 4 * D], F32, kind="ExternalInput").ap()
        hy_fw1 = dt("hy_f_w1", [33, 64], F32, kind="ExternalInput").ap()
        hy_fw2 = dt("hy_f_w2", [64, 64], F32, kind="ExternalInput").ap()
        hy_fw3 = dt("hy_f_w3", [64, 64], F32, kind="ExternalInput").ap()
        feat_d = dt("featT", [33, T], F32, kind="ExternalInput").ap()
        winf_d = dt("winf", [T, D], F32, kind="ExternalInput").ap()
        winb_d = dt("winb", [T, D], F32, kind="ExternalInput").ap()
        fwd_d = dt("fwdtab", [4, 128, 4096], BF16, kind="ExternalInput").ap()
        inv_d = dt("invtab", [4, 128, 4096], BF16, kind="ExternalInput").ap()

    from contextlib import ExitStack

    with ExitStack() as stack:
        ec = stack.enter_context
        S = Sched(nc, stack)
        sb = lambda name, shape, dtype: ec(nc.sbuf_tensor(name, shape, dtype))
        x = sb("x", [128, NDC, T], F32)
        h = sb("h", [128, NDC, T], BF16)
        slots = [sb("slot%d" % i, [128, SLOT_ELEMS], BF16) for i in range(NSLOT)]
        pv = sb("pv", [128, PV_N], F32)
        ident = sb("ident_sb", [128, 128], F32)
        identb = sb("identb_sb", [128, 128], BF16)
        ones_bf = sb("ones_bf", [128, 128], BF16)
        consts = sb("consts", [128, 4], F32)
        modT = [sb("modT%d" % l, [128, 96], F32) for l in range(2)]
        nA = sb("nA", [128, NDC], F32)
        condb = sb("condb", [128, NDC], BF16)
        arena_t = sb("arena", [128, AW], F32)
        fm1s = sb("fm1s", [64, 1024], F32) if os.environ.get("HY_DBG") else None
        A = Arena(arena_t, AW)
        ps = ec(nc.psum_tensor("ps", [128, 4096], F32))

        def bank(b, n=512):
            return ps[:, b * 512:b * 512 + n]

        def bankT(b, n=1024):
            return ps[:, b * 512:(b + 1) * 512].bitcast(BF16)[:, 0:n]

        def PB(b):
            return ("ps", b)

        def pcol(name, j=0, n=1):
            o, _ = PV_OFF[name]
            return pv[:, o + j:o + j + n]

        slot_i = [0]

        def stream_w(src, kc, ncols, q="pool", flat=False):
            i = slot_i[0] % NSLOT
            slot_i[0] += 1
            view = slots[i][:, 0:kc * ncols].rearrange("p (k n) -> p k n", n=ncols)
            if flat:
                S.dma(q, slots[i][:, 0:kc * ncols], src, wr=[("slot", i)])
            else:
                S.dma(q, view, src, wr=[("slot", i)])
            return view, ("slot", i)

        S.dma("sp", pv[:, :], pvec_d, wr=["pv"])
        S.dma("sp", ident[:, :], ident_d, wr=["ident"])
        S.dma("pool", identb[:, :], ident_d, wr=["identb"])
        S.op("dve", lambda b: b.memset(ones_bf[:, :], 1.0), wr=["ones"])
        S.op("dve", lambda b: b.memset(consts[:, 0:1], EPS), wr=["consts"])
        S.op("dve", lambda b: b.memset(consts[:, 1:2], 1.0), wr=["consts"])

        A.reset()
        tmpA = A.f32(2048)
        tmpB = A.f32(2048)
        tmps = [(tmpA, "tmpA"), (tmpB, "tmpB")]
        for tc in range(8):
            tm, tk = tmps[tc % 2]
            S.dma("sp", tm[:, :], xin[tc * 128:(tc + 1) * 128, :], wr=[tk])
            for g in range(4):
                b = (tc * 4 + g) % 8
                for j in range(4):
                    dc = g * 4 + j
                    S.op("pe", lambda e, b=b, j=j, tm=tm, dc=dc: e.transpose(
                        bank(b)[:, j * 128:(j + 1) * 128], tm[:, dc * 128:(dc + 1) * 128], ident[:, :]),
                        rd=[tk, "ident"], wr=[PB(b)], tick=(j == 3))
                outv = x[:, g * 4:(g + 1) * 4, tc * 128:(tc + 1) * 128]
                inv = bank(b).rearrange("p (j t) -> p j t", t=128)
                wrl = [("x", g * 4 + j) for j in range(4)]
                if g % 2 == 0:
                    S.op("act", lambda e, outv=outv, inv=inv: e.copy(outv, inv), rd=[PB(b)], wr=wrl)
                else:
                    S.op("dve", lambda e, outv=outv, inv=inv: e.tensor_copy(outv, inv), rd=[PB(b)], wr=wrl)

        S.op("act", lambda e: e.activation(condb[:, :], pcol("cond", 0, 16), AF.Silu), rd=["pv"], wr=["condb"])
        mod_queue = [(l, b) for l in range(2) for b in range(48)]

        mod_bank = [7]

        def mod_block(l, blk):
            mb = mod_bank[0]
            wl = w_ada[l].rearrange("(kc p) n -> p kc n", p=128)
            view, sk = stream_w(wl[:, :, blk * 256:(blk + 1) * 256], 16, 256)
            for cc in range(2):
                for kc in range(16):
                    S.op("pe", lambda e, view=view, cc=cc, kc=kc, mb=mb: e.matmul(
                        ps[:, mb * 512 + cc:mb * 512 + cc + 1], lhsT=view[:, kc, cc * 128:(cc + 1) * 128],
                        rhs=condb[:, kc:kc + 1], start=(kc == 0), stop=(kc == 15)),
                        rd=[sk, "condb"], wr=[PB(mb)], tick=(kc == 15))
            o, _ = PV_OFF["b_ada%d" % l]
            c0 = 2 * blk
            S.op("dve", lambda e, l=l, o=o, c0=c0, mb=mb: e.tensor_tensor(modT[l][:, c0:c0 + 2], ps[:, mb * 512:mb * 512 + 2],
                                                                          pv[:, o + c0:o + c0 + 2], ALU.add),
                 rd=[PB(mb), "pv"], wr=[("mod", l, c0 // 16)])

        def mod_run(n):
            for _ in range(n):
                if mod_queue:
                    l, b = mod_queue.pop(0)
                    mod_block(l, b)

        def mod_flush(l, nblk=48):
            while mod_queue and (mod_queue[0][0] < l or (mod_queue[0][0] == l and mod_queue[0][1] < nblk)):
                mod_run(1)

        if use_hy:
            mod_run(16)
        else:
            mod_run(96)

        def mod(l, which, dc):
            return modT[l][:, which * 16 + dc:which * 16 + dc + 1]

        def compute_rstd(rstd, rtmp):
            for dc in range(NDC):
                S.op("act", lambda e, dc=dc: e.activation(h[:, dc, :], x[:, dc, :], AF.Square),
                     rd=[("x", dc)], wr=[("h", dc)])
            for half in range(2):
                for dc in range(NDC):
                    S.op("pe", lambda e, dc=dc, half=half: e.matmul(
                        bank(half), lhsT=ones_bf[:, :], rhs=h[:, dc, half * 512:(half + 1) * 512],
                        start=(dc == 0), stop=(dc == NDC - 1)),
                        rd=["ones", ("h", dc)], wr=[PB(half)], tick=(dc == NDC - 1))
            S.op("act", lambda e: e.activation(rtmp, ps[:, 0:1024], AF.Sqrt, bias=consts[:, 0:1], scale=1.0 / D),
                 rd=[PB(0), PB(1), "consts"], wr=["rtmp"])
            S.op("dve", lambda e: e.reciprocal(rstd, rtmp), rd=["rtmp"], wr=["rstd"])

        def norm_mod(l, which):
            mod_flush(l, 16 if which == 0 else 48)
            S.barrier()
            A.reset()
            tA = A.f32(1024)
            tB = A.f32(1024)
            rstd = A.f32(1024)
            rtmp = A.f32(1024)
            tl = [(tA, "tmpA"), (tB, "tmpB")]
            go, _ = PV_OFF["ng%d_%d" % (l, which)]
            sc0 = (1 if which == 0 else 4) * 16
            sh0 = (0 if which == 0 else 3) * 16
            compute_rstd(rstd, rtmp)
            S.op("dve", lambda e: e.scalar_tensor_tensor(nA[:, :], modT[l][:, sc0:sc0 + 16], 1.0, pv[:, go:go + 16],
                                                         ALU.add, ALU.mult), rd=[("mod", l, sc0 // 16), "pv"], wr=["nA"])
            for dc in range(NDC):
                tm, tk = tl[dc % 2]
                S.op("dve", lambda e, dc=dc, tm=tm: e.scalar_tensor_tensor(
                    tm, x[:, dc, :], nA[:, dc:dc + 1], rstd, ALU.mult, ALU.mult),
                    rd=[("x", dc), "nA", "rstd"], wr=[tk])
                S.op("act", lambda e, dc=dc, tm=tm: e.activation(
                    h[:, dc, :], tm, AF.Identity, bias=modT[l][:, sh0 + dc:sh0 + dc + 1]),
                    rd=[tk, ("mod", l, sh0 // 16)], wr=[("h", dc)])

        def mlp(l):
            S.barrier()
            A.reset()
            hid = A.bf16(16 * T).rearrange("p (j t) -> p j t", t=T)
            tA = A.f32(512)
            tB = A.f32(512)
            tl = [(tA, "tmpA"), (tB, "tmpB")]
            w1 = mlp_w1[l].rearrange("(kc p) n -> p kc n", p=128)
            w2 = mlp_w2[l].rearrange("(kc p) n -> p kc n", p=128)
            pb = [0]
            for q in range(4):
                for blk in range(8):
                    c0 = q * 2048 + blk * 256
                    if blk % 4 == 0 and l == 0:
                        mod_run(1)
                    view, sk = stream_w(w1[:, :, c0:c0 + 256], 16, 256)
                    for cc in range(2):
                        jc = blk * 2 + cc
                        for half in range(2):
                            b = pb[0] % 6
                            pb[0] += 1
                            for kc in range(16):
                                S.op("pe", lambda e, view=view, cc=cc, kc=kc, half=half, b=b: e.matmul(
                                    bank(b), lhsT=view[:, kc, cc * 128:(cc + 1) * 128],
                                    rhs=h[:, kc, half * 512:(half + 1) * 512], start=(kc == 0), stop=(kc == 15)),
                                    rd=[sk, ("h", kc)], wr=[PB(b)], tick=(kc == 15))
                            tm, tk = tl[(pb[0]) % 2]
                            S.op("act", lambda e, b=b, tm=tm: e.activation(tm, bank(b), AF.Relu), rd=[PB(b)], wr=[tk])
                            S.op("dve", lambda e, jc=jc, half=half, tm=tm: e.tensor_tensor(
                                hid[:, jc, half * 512:(half + 1) * 512], tm, tm, ALU.mult), rd=[tk], wr=[("hid", jc)])
                for blk in range(8):
                    if blk % 4 == 0 and l == 0:
                        mod_run(1)
                    view, sk = stream_w(w2[:, q * 16:(q + 1) * 16, blk * 256:(blk + 1) * 256], 16, 256)
                    for cc in range(2):
                        dc = blk * 2 + cc
                        for half in range(2):
                            b = pb[0] % 6
                            pb[0] += 1
                            for jc in range(16):
                                S.op("pe", lambda e, view=view, cc=cc, jc=jc, half=half, b=b: e.matmul(
                                    bank(b), lhsT=view[:, jc, cc * 128:(cc + 1) * 128],
                                    rhs=hid[:, jc, half * 512:(half + 1) * 512], start=(jc == 0), stop=(jc == 15)),
                                    rd=[sk, ("hid", jc)], wr=[PB(b)], tick=(jc == 15))
                            S.op("dve", lambda e, dc=dc, half=half, b=b: e.scalar_tensor_tensor(
                                x[:, dc, half * 512:(half + 1) * 512], bank(b), mod(l, 5, dc),
                                x[:, dc, half * 512:(half + 1) * 512], ALU.mult, ALU.add),
                                rd=[PB(b), ("mod", l, 5), ("x", dc)], wr=[("x", dc)])

        def retention(l):
            mod_bank[0] = 6
            S.barrier()
            A.reset()
            qT = A.bf16(2048).rearrange("p (c t) -> p c t", t=T)
            kT = A.bf16(2048).rearrange("p (c t) -> p c t", t=T)
            ktok = A.bf16(2048).rearrange("p (n d) -> p n d", d=256)
            vv = A.bf16(4096).rearrange("p (n v) -> p n v", v=512)
            ga = A.bf16(4096).rearrange("p (n v) -> p n v", v=512)
            Sb = A.bf16(8192).rearrange("p (n c v) -> p n c v", c=2, v=512)
            Sraw = A.f32(1024).rearrange("p (c v) -> p c v", v=512)
            Sfb = A.bf16(2048).rearrange("p (k c v) -> p k c v", k=2, c=2)
            scb = A.bf16(256).rearrange("p (k i) -> p k i", i=128)
            qtl = A.bf16(1024).rearrange("p (k a i) -> p k a i", k=2, a=4)
            scrA = A.f32(512)
            scrB = A.f32(512)
            ygb = scrB.bitcast(BF16).rearrange("p (k v) -> p k v", v=512)
            ygT = A.bf16(4096).rearrange("p (c t) -> p c t", t=T)
            maskT = A.f32(128)
            mt = A.f32(256)
            qd = A.f32(256).rearrange("p (a i) -> p a i", i=128)
            gng = A.f32(512)
            rot = A.f32(2048).rearrange("p (a t) -> p a t", t=T)
            rc = A.f32(RC_N)
            lg = A.f32(16)
            kd = A.f32(16)
            cd = A.f32(16)
            cde = A.f32(16)
            st = A.f32(16)
            Dpos, Cpos, Dneg, Cneg, Ipos1, Ineg = [rc[:, i * 128:(i + 1) * 128] for i in range(6)]
            Jneg = rc[:, 768:769]
            Jpos = rc[:, 769:770]
            flag = pcol("flags", 0)

            S.dma("sp", rc, rc_d, wr=["rc"])
            S.dma("sp", rot, rot_d, wr=["rot"])
            dco, _ = PV_OFF["ret_decay"]
            S.op("act", lambda e: e.activation(lg, pv[:, dco:dco + 16], AF.Exp, scale=-1.0), rd=["pv"], wr=["lg"])
            S.op("act", lambda e: e.activation(lg, lg, AF.Ln, bias=consts[:, 1:2]), rd=["lg", "consts"], wr=["lg"])
            S.op("dve", lambda e: e.tensor_scalar(lg, lg, -1.0, None, ALU.mult), rd=["lg"], wr=["lg"])
            S.op("act", lambda e: e.activation(cd, lg, AF.Exp, scale=128.0), rd=["lg"], wr=["cd"])
            S.op("dve", lambda e: e.tensor_scalar(cde, cd, flag, None, ALU.mult), rd=["cd", "pv"], wr=["cde"])
            S.op("act", lambda e: e.activation(kd[:, 0:8], lg[:, 0:8], AF.Exp, scale=Jneg), rd=["lg", "rc"], wr=["kd"])
            S.op("act", lambda e: e.activation(kd[:, 8:16], lg[:, 8:16], AF.Exp, scale=Jpos), rd=["lg", "rc"], wr=["kd"])
            S.op("dve", lambda e: e.tensor_scalar(kd, kd, 1.0 / 16.0, None, ALU.mult), rd=["kd"], wr=["kd"])

            wq = w_qkvg.rearrange("(kc p) n -> p kc n", p=128)
            gpb = [0]

            def gbank():
                b = gpb[0] % 3
                gpb[0] += 1
                return b

            def ktok_build(hd, col):
                for tc in range(8):
                    for c in range(2):
                        S.op("pe", lambda e, tc=tc, c=c: e.transpose(
                            bankT(4)[:, c * 128:(c + 1) * 128], kT[:, c, tc * 128:(tc + 1) * 128], identb[:, :]),
                            rd=["kT", "identb"], wr=[PB(4)], tick=(c == 1))
                    S.op("act", lambda e, tc=tc: e.activation(ktok[:, tc, :], bankT(4)[:, 0:256], AF.Identity,
                                                              scale=kd[:, col:col + 1]),
                         rd=[PB(4), "kd"], wr=[("ktok", tc)])

            def state_delta(n):
                for c in range(2):
                    b = 3 if c == 0 else 7
                    S.op("pe", lambda e, n=n, c=c, b=b: e.matmul(
                        bank(b), lhsT=ktok[:, n, c * 128:(c + 1) * 128], rhs=vv[:, n, :], start=True, stop=True),
                        rd=[("ktok", n), "vv"], wr=[PB(b)])

            def state_update(scal):
                for c in range(2):
                    b = 3 if c == 0 else 7
                    S.op("dve", lambda e, c=c, b=b: e.scalar_tensor_tensor(
                        Sraw[:, c, :], Sraw[:, c, :], scal, bank(b), ALU.mult, ALU.add),
                        rd=[PB(b), "Sraw", "cd", "cde"], wr=["Sraw"])

            for hd in range(8):
                S.op("act", lambda e, hd=hd: e.activation(mt[:, 0:128], Dpos, AF.Exp, scale=lg[:, hd:hd + 1]),
                     rd=["rc", "lg"], wr=["mt"])
                S.op("act", lambda e, hd=hd: e.activation(mt[:, 128:256], Dneg, AF.Exp, scale=lg[:, 8 + hd:9 + hd]),
                     rd=["rc", "lg"], wr=["mt"])
                S.op("dve", lambda e: e.tensor_tensor(mt[:, 0:128], mt[:, 0:128], Cpos, ALU.mult), rd=["mt", "rc"], wr=["mt"])
                S.op("dve", lambda e: e.tensor_tensor(mt[:, 128:256], mt[:, 128:256], Cneg, ALU.mult), rd=["mt", "rc"], wr=["mt"])
                S.op("dve", lambda e: e.tensor_tensor(maskT, mt[:, 0:128], mt[:, 128:256], ALU.add), rd=["mt"], wr=["maskT"])
                S.op("act", lambda e, hd=hd: e.activation(qd[:, 0, :], Ipos1, AF.Exp, scale=lg[:, hd:hd + 1]),
                     rd=["rc", "lg"], wr=["qd"])
                S.op("act", lambda e, hd=hd: e.activation(qd[:, 1, :], Ineg, AF.Exp, scale=lg[:, 8 + hd:9 + hd]),
                     rd=["rc", "lg"], wr=["qd"])
                S.dma("sp", gng, gng_d[:, hd * 512:(hd + 1) * 512], wr=["gng"])

                wviews = {}

                def qk_half(qi, half, hd=hd):
                    dst, dk = ((qT, "qT"), (kT, "kT"))[qi]
                    if half == 0:
                        c0 = qi * 2048 + hd * 256
                        wviews[("qk", qi)] = stream_w(wq[:, :, c0:c0 + 256], 16, 256)
                    view, sk = wviews[("qk", qi)]
                    for cc in range(2):
                        for kc in range(16):
                            S.op("pe", lambda e, view=view, cc=cc, kc=kc, half=half: e.matmul(
                                bank(cc), lhsT=view[:, kc, cc * 128:(cc + 1) * 128],
                                rhs=h[:, kc, half * 512:(half + 1) * 512], start=(kc == 0), stop=(kc == 15)),
                                rd=[sk, ("h", kc)], wr=[PB(cc)], tick=(kc == 15))
                    hs = slice(half * 512, (half + 1) * 512)
                    cosv = rot[:, 0, hs]
                    sinv = rot[:, 1, hs]
                    S.op("dve", lambda e, cosv=cosv: e.tensor_tensor(scrA, bank(0), cosv, ALU.mult),
                         rd=[PB(0), "rot"], wr=["scrA"])
                    S.op("dve", lambda e, sinv=sinv: e.tensor_tensor(scrB, bank(1), sinv, ALU.mult),
                         rd=[PB(1), "rot"], wr=[("ygb", 0), ("ygb", 1)])
                    S.op("dve", lambda e, dst=dst, hs=hs: e.tensor_tensor(dst[:, 0, hs], scrA, scrB, ALU.subtract),
                         rd=["scrA", ("ygb", 0), ("ygb", 1)], wr=[dk])
                    S.op("dve", lambda e, sinv=sinv: e.tensor_tensor(scrA, bank(0), sinv, ALU.mult),
                         rd=[PB(0), "rot"], wr=["scrA"])
                    S.op("dve", lambda e, cosv=cosv: e.tensor_tensor(scrB, bank(1), cosv, ALU.mult),
                         rd=[PB(1), "rot"], wr=[("ygb", 0), ("ygb", 1)])
                    S.op("dve", lambda e, dst=dst, hs=hs: e.tensor_tensor(dst[:, 1, hs], scrA, scrB, ALU.add),
                         rd=["scrA", ("ygb", 0), ("ygb", 1)], wr=[dk])

                def vg_group(vi, piece, tc, hd=hd):
                    dst, dk = ((vv, "vv"), (ga, "ga"))[vi]
                    if tc == 0:
                        c0 = 4096 + vi * 4096 + hd * 512 + piece * 256
                        wviews[("vg", vi, piece)] = stream_w(wq[:, :, c0:c0 + 256], 16, 256)
                    view, sk = wviews[("vg", vi, piece)]
                    b = gbank()
                    for kc in range(16):
                        S.op("pe", lambda e, view=view, kc=kc, tc=tc, b=b: e.matmul(
                            bank(b, 256), lhsT=h[:, kc, tc * 128:(tc + 1) * 128], rhs=view[:, kc, :],
                            start=(kc == 0), stop=(kc == 15)),
                            rd=[sk, ("h", kc)], wr=[PB(b)], tick=(kc == 15))
                    fn = AF.Copy if vi == 0 else AF.Silu
                    S.op("act", lambda e, dst=dst, tc=tc, piece=piece, b=b, fn=fn: e.activation(
                        dst[:, tc, piece * 256:(piece + 1) * 256], bank(b, 256), fn),
                        rd=[PB(b)], wr=[dk])
                    if vi == 1:
                        S.op("dve", lambda e, dst=dst, tc=tc, piece=piece: e.tensor_tensor(
                            dst[:, tc, piece * 256:(piece + 1) * 256], dst[:, tc, piece * 256:(piece + 1) * 256],
                            gng[:, piece * 256:(piece + 1) * 256], ALU.mult), rd=[dk, "gng"], wr=[dk])

                qk_half(1, 0)
                qk_half(1, 1)
                for piece in range(2):
                    for tc in range(8):
                        vg_group(0, piece, tc)
                extras = [lambda: qk_half(0, 0), lambda: qk_half(0, 1)]
                for piece in range(2):
                    for tc in range(8):
                        extras.append(lambda piece=piece, tc=tc: vg_group(1, piece, tc))
                for _ in range(8 if hd == 0 else (4 if hd <= 6 else 0)):
                    extras.append(lambda: mod_run(1))
                ktok_build(hd, 8 + hd)
                S.dma("sp", Sraw, state_in[1, hd].rearrange("(c p) v -> p c v", p=128), wr=["Sraw"])
                S.op("act", lambda e: e.copy(Sb[:, 7], Sraw), rd=["Sraw"], wr=[("Sb", 7)])
                for n in range(7, -1, -1):
                    state_delta(n)
                    scal = cde[:, 8 + hd:9 + hd] if n in (1, 3, 5) else cd[:, 8 + hd:9 + hd]
                    state_update(scal)
                    if n >= 1:
                        if (n - 1) in (1, 3, 5):
                            S.op("act", lambda e, n=n: e.activation(Sb[:, n - 1], Sraw, AF.Identity, scale=flag),
                                 rd=["Sraw", "pv"], wr=[("Sb", n - 1)])
                        else:
                            S.op("act", lambda e, n=n: e.copy(Sb[:, n - 1], Sraw), rd=["Sraw"], wr=[("Sb", n - 1)])
                    if n % 2 == 0:
                        S.dma("sp", nstate[n // 2, 1, hd].rearrange("(c p) v -> p c v", p=128), Sraw, rd=["Sraw"])
                    for _ in range(3 if n >= 6 else 2):
                        if extras:
                            extras.pop(0)()
                while extras:
                    extras.pop(0)()
                ktok_build(hd, hd)
                S.dma("sp", Sraw, state_in[0, hd].rearrange("(c p) v -> p c v", p=128), wr=["Sraw"])
                S.op("act", lambda e: e.copy(Sfb[:, 0], Sraw), rd=["Sraw"], wr=[("Sfb", 0)])

                def fwd_front(n, hd=hd):
                    cur = n % 2
                    ob = 6 if cur == 0 else 2
                    ns = slice(n * 128, (n + 1) * 128)
                    for c in range(2):
                        S.op("pe", lambda e, c=c, ns=ns: e.matmul(
                            bank(5, 128), lhsT=kT[:, c, ns], rhs=qT[:, c, ns], start=(c == 0), stop=(c == 1)),
                            rd=["kT", "qT"], wr=[PB(5)], tick=(c == 1))
                    S.op("dve", lambda e, cur=cur: e.tensor_tensor(scb[:, cur, :], bank(5, 128), maskT, ALU.mult),
                         rd=[PB(5), "maskT"], wr=[("scb", cur)])
                    for a in range(4):
                        S.op("pool", lambda e, a=a, cur=cur, ns=ns: e.tensor_tensor(
                            qtl[:, cur, a, :], qT[:, a % 2, ns], qd[:, a // 2, :], ALU.mult),
                            rd=["qT", "qd"], wr=[("qtl", cur)])
                    S.op("pe", lambda e, cur=cur, n=n, ob=ob: e.matmul(bank(ob), lhsT=scb[:, cur, :], rhs=vv[:, n, :],
                                                                       start=True, stop=False),
                         rd=[("scb", cur), "vv"], wr=[PB(ob)], tick=False)
                    for c in range(2):
                        S.op("pe", lambda e, cur=cur, c=c, ob=ob: e.matmul(bank(ob), lhsT=qtl[:, cur, c, :], rhs=Sfb[:, cur, c, :],
                                                                           start=False, stop=False),
                             rd=[("qtl", cur), ("Sfb", cur)], wr=[PB(ob)], tick=False)
                    for c in range(2):
                        S.op("pe", lambda e, cur=cur, c=c, n=n, ob=ob: e.matmul(bank(ob), lhsT=qtl[:, cur, 2 + c, :], rhs=Sb[:, n, c, :],
                                                                                start=False, stop=(c == 1)),
                             rd=[("qtl", cur), ("Sb", n)], wr=[PB(ob)], tick=(c == 1))
                    state_delta(n)
                    scal = cde[:, hd:hd + 1] if n in (2, 4, 6) else cd[:, hd:hd + 1]
                    state_update(scal)
                    if n < 7:
                        if (n + 1) in (2, 4, 6):
                            S.op("act", lambda e, cur=cur: e.activation(Sfb[:, 1 - cur], Sraw, AF.Identity, scale=flag),
                                 rd=["Sraw", "pv"], wr=[("Sfb", 1 - cur)])
                        else:
                            S.op("act", lambda e, cur=cur: e.copy(Sfb[:, 1 - cur], Sraw), rd=["Sraw"], wr=[("Sfb", 1 - cur)])
                    if n % 2 == 1:
                        S.dma("sp", nstate[n // 2, 0, hd].rearrange("(c p) v -> p c v", p=128), Sraw, rd=["Sraw"])

                def fwd_back(n):
                    cur = n % 2
                    ob = 6 if cur == 0 else 2
                    ns = slice(n * 128, (n + 1) * 128)
                    S.op("dve", lambda e, ob=ob: e.bn_stats(st[:, 0:6], bank(ob)), rd=[PB(ob)], wr=["st"])
                    S.op("dve", lambda e: e.bn_aggr(st[:, 6:8], st[:, 0:6]), rd=["st"], wr=["st"])
                    S.op("act", lambda e: e.activation(st[:, 8:9], st[:, 7:8], AF.Sqrt, bias=consts[:, 0:1]),
                         rd=["st", "consts"], wr=["st"])
                    S.op("dve", lambda e: e.reciprocal(st[:, 9:10], st[:, 8:9]), rd=["st"], wr=["st"])
                    S.op("dve", lambda e, ob=ob: e.tensor_scalar(scrA, bank(ob), st[:, 6:7], st[:, 9:10], ALU.subtract, ALU.mult),
                         rd=[PB(ob), "st"], wr=["scrA"])
                    S.op("dve", lambda e, cur=cur, n=n: e.tensor_tensor(ygb[:, cur, :], scrA, ga[:, n, :], ALU.mult),
                         rd=["scrA", "ga"], wr=[("ygb", cur)])
                    for cc in range(4):
                        S.op("pe", lambda e, cc=cc, cur=cur: e.transpose(
                            bankT(4)[:, cc * 128:(cc + 1) * 128], ygb[:, cur, cc * 128:(cc + 1) * 128], identb[:, :]),
                            rd=[("ygb", cur), "identb"], wr=[PB(4)], tick=(cc == 3))
                    S.op("act", lambda e, ns=ns: e.copy(ygT[:, :, ns], bankT(4)[:, 0:512].rearrange("p (c t) -> p c t", t=128)),
                         rd=[PB(4)], wr=["ygT"])

                for n in range(8):
                    fwd_front(n)
                    if n > 0:
                        fwd_back(n - 1)
                fwd_back(7)
                wo = w_o[hd * 512:(hd + 1) * 512, :].rearrange("(c p) n -> p c n", p=128)
                for piece in range(2):
                    view, sk = stream_w(wo[:, :, piece * 1024:(piece + 1) * 1024], 4, 1024)
                    for dcl in range(8):
                        dc = piece * 8 + dcl
                        for half in range(2):
                            b = gbank()
                            for c in range(4):
                                S.op("pe", lambda e, view=view, c=c, dcl=dcl, half=half, b=b: e.matmul(
                                    bank(b), lhsT=view[:, c, dcl * 128:(dcl + 1) * 128],
                                    rhs=ygT[:, c, half * 512:(half + 1) * 512], start=(c == 0), stop=(c == 3)),
                                    rd=[sk, "ygT"], wr=[PB(b)], tick=(c == 3))
                            S.op("dve", lambda e, dc=dc, half=half, b=b: e.scalar_tensor_tensor(
                                x[:, dc, half * 512:(half + 1) * 512], bank(b), mod(l, 2, dc),
                                x[:, dc, half * 512:(half + 1) * 512], ALU.mult, ALU.add),
                                rd=[PB(b), ("mod", l, 2), ("x", dc)], wr=[("x", dc)])

        def hyena(l):
            S.barrier()
            A.reset()
            h3b = A.bf16(1024)
            fm1 = A.f32(1024)
            fm2 = A.f32(1024)
            fm3 = A.f32(1024)
            featT = A.f32(1024)
            fw = [A.f32(64) for _ in range(3)]
            fbias = A.f32(4)
            S.dma("sp", featT[0:33, :], feat_d, wr=["featT"])
            S.dma("sp", fw[0][0:33, :], hy_fw1, wr=[("fw", 0)])
            S.dma("sp", fw[1][0:64, :], hy_fw2, wr=[("fw", 1)])
            S.dma("sp", fw[2][0:64, :], hy_fw3, wr=[("fw", 2)])
            for j in range(3):
                S.op("dve", lambda e, j=j: e.tensor_tensor(fbias[0:64, j:j + 1], pcol("hy_fb%d" % j)[0:64, :],
                                                           pcol("hy_ff%d" % j)[0:64, :], ALU.mult),
                     rd=["pv"], wr=["fbias"])
            for j in range(3):
                kk = 33 if j == 0 else 64
                src = featT if j == 0 else fm1
                for half in range(2):
                    S.op("pe", lambda e, j=j, kk=kk, src=src, half=half: e.matmul(
                        bank(half)[0:64, :], lhsT=fw[j][0:kk, 0:64], rhs=src[0:kk, half * 512:(half + 1) * 512],
                        start=True, stop=True), rd=[("fw", j), "featT", "fm1"], wr=[PB(half)])
                S.op("act", lambda e, j=j: e.activation(fm2[0:64, :], ps[0:64, 0:1024], AF.Identity,
                                                        scale=pcol("hy_ff%d" % j)[0:64, :], bias=fbias[0:64, j:j + 1]),
                     rd=[PB(0), PB(1), "pv", "fbias"], wr=["fm2"])
                S.op("dve", lambda e: e.tensor_scalar(fm3[0:64, :], fm2[0:64, :], PI, None, ALU.is_gt), rd=["fm2"], wr=["fm3"])
                S.op("dve", lambda e: e.scalar_tensor_tensor(fm2[0:64, :], fm3[0:64, :], -2.0 * PI, fm2[0:64, :], ALU.mult, ALU.add),
                     rd=["fm3", "fm2"], wr=["fm2"])
                S.op("dve", lambda e: e.tensor_scalar(fm3[0:64, :], fm2[0:64, :], -PI, None, ALU.is_lt), rd=["fm2"], wr=["fm3"])
                S.op("dve", lambda e: e.scalar_tensor_tensor(fm2[0:64, :], fm3[0:64, :], 2.0 * PI, fm2[0:64, :], ALU.mult, ALU.add),
                     rd=["fm3", "fm2"], wr=["fm2"])
                S.op("act", lambda e: e.activation(fm1[0:64, :], fm2[0:64, :], AF.Sin), rd=["fm2"], wr=["fm1"])
            S.op("act", lambda e: e.copy(h3b[0:64, :], fm1[0:64, :]), rd=["fm1"], wr=["h3b"])
            if dbg_out is not None:
                S.op("act", lambda e: e.copy(fm1s[0:64, :], fm1[0:64, :]), rd=["fm1"], wr=["fm1s"])

            HY_CUT = int(os.environ.get('HY_CUT', '99'))
            if HY_CUT <= 1:
                return
            S.barrier()
            A.reset()
            h3b = A.bf16(1024)
            Pb = [A.f32(1024) for _ in range(2)]
            ACC0 = A.f32(1024)
            ACCb = [ACC0, ACC0]
            vcb = [A.bf16(1024) for _ in range(2)]
            x1cb = [A.bf16(1024) for _ in range(2)]
            x2cb = [A.bf16(1024) for _ in range(2)]
            Rre = A.bf16(2048).rearrange("p (t c) -> p t c", c=256)
            Rim = A.bf16(2048).rearrange("p (t c) -> p t c", c=256)
            SPre = A.f32(2048).rearrange("p (f c) -> p f c", c=256)
            SPim = A.f32(2048).rearrange("p (f c) -> p f c", c=256)
            Y = A.bf16(2048).rearrange("p (f c) -> p f c", c=128)
            z1 = A.bf16(1024)
            z2g = A.bf16(4096).rearrange("p (b t) -> p b t", t=T)
            t1 = A.f32(1024)
            t2 = A.f32(1024)
            t1v = t1.rearrange("p (t c) -> p t c", c=128)
            t2v = t2.rearrange("p (t c) -> p t c", c=128)
            winf = A.f32(1024).rearrange("p (t c) -> p t c", c=128)
            winb = A.f32(1024).rearrange("p (t c) -> p t c", c=128)
            woutb = A.bf16(512).rearrange("p (g c) -> p g c", c=128)
            w0n = A.f32(48)
            w2n = A.f32(48)
            gb = A.f32(16)
            fm1o = pcol("flags", 1)
            o0, _ = PV_OFF["hy_cw0"]
            o2, _ = PV_OFF["hy_cw2"]
            S.op("dve", lambda e: e.tensor_scalar(w0n, pv[:, o0:o0 + 48], fm1o, None, ALU.mult), rd=["pv"], wr=["w0n"])
            S.op("dve", lambda e: e.tensor_scalar(w2n, pv[:, o2:o2 + 48], fm1o, None, ALU.mult), rd=["pv"], wr=["w2n"])

            win = hy_w_in.rearrange("(kc p) n -> p kc n", p=128)
            wff = hy_wout_f.rearrange("k (g c) -> k g c", c=D)
            winf_v = winf_d.rearrange("(t p) c -> p t c", p=128)
            winb_v = winb_d.rearrange("(t p) c -> p t c", p=128)
            opb = [0]
            pacc = [0]

            def transpose_to_R(src, sk):
                for tc in range(8):
                    S.op("pe", lambda e, tc=tc: e.transpose(
                        bankT(2)[:, tc * 128:(tc + 1) * 128], src[:, tc * 128:(tc + 1) * 128], identb[:, :]),
                        rd=[sk, "identb"], wr=[PB(2)], tick=(tc == 7))
                pv3 = bankT(2)[:, 0:1024].rearrange("p (t c) -> p t c", c=128)
                S.op("act", lambda e: e.copy(Rre[:, :, 128:256], pv3), rd=[PB(2)], wr=["RreU"])
                S.op("dve", lambda e: e.tensor_copy(Rim[:, :, 128:256], Rre[:, :, 128:256]), rd=["RreU"], wr=["RimU"])

            def stageA(blk, tis=(0, 1, 2)):
                s = blk % 2
                for ti, (dst, dk) in enumerate(((vcb[s], ("vc", s)), (x1cb[s], ("x1c", s)), (x2cb[s], ("x2c", s)))):
                    if ti not in tis:
                        continue
                    c0 = ti * 2048 + blk * 128
                    col = ti * 16 + blk
                    pi = pacc[0] % 2
                    pacc[0] += 1
                    P, ACC, pk, ak = Pb[pi], ACCb[pi], ("P", pi), "ACC"
                    view, sk = stream_w(win[:, :, c0:c0 + 128], 16, 128)
                    for half in range(2):
                        for kc in range(16):
                            S.op("pe", lambda e, view=view, kc=kc, half=half: e.matmul(
                                bank(half), lhsT=view[:, kc, :], rhs=h[:, kc, half * 512:(half + 1) * 512],
                                start=(kc == 0), stop=(kc == 15)),
                                rd=[sk, ("h", kc)], wr=[PB(half)], tick=(kc == 15))
                    S.op("act", lambda e, col=col, P=P: e.activation(P, ps[:, 0:1024], AF.Identity, bias=pcol("hy_b_in", col)),
                         rd=[PB(0), PB(1), "pv"], wr=[pk])
                    S.op("act", lambda e, col=col, P=P, ACC=ACC: e.activation(ACC, P, AF.Identity, scale=pcol("hy_cw1", col),
                                                                              bias=pcol("hy_cb", col)), rd=[pk, "pv"], wr=[ak])
                    S.op("dve", lambda e, col=col, P=P, ACC=ACC: e.scalar_tensor_tensor(
                        ACC[:, 1:T], P[:, 0:T - 1], pcol("hy_cw0", col), ACC[:, 1:T], ALU.mult, ALU.add),
                        rd=[pk, ak, "pv"], wr=[ak])
                    S.op("dve", lambda e, col=col, P=P, ACC=ACC: e.scalar_tensor_tensor(
                        ACC[:, 0:T - 1], P[:, 1:T], pcol("hy_cw2", col), ACC[:, 0:T - 1], ALU.mult, ALU.add),
                        rd=[pk, ak, "pv"], wr=[ak])
                    S.op("dve", lambda e, col=col, P=P, ACC=ACC: e.scalar_tensor_tensor(
                        ACC[:, 256:1024:256], P[:, 255:1023:256], w0n[:, col:col + 1], ACC[:, 256:1024:256], ALU.mult, ALU.add),
                        rd=[pk, ak, "w0n"], wr=[ak])
                    S.op("dve", lambda e, col=col, P=P, ACC=ACC: e.scalar_tensor_tensor(
                        ACC[:, 255:1023:256], P[:, 256:1024:256], w2n[:, col:col + 1], ACC[:, 255:1023:256], ALU.mult, ALU.add),
                        rd=[pk, ak, "w2n"], wr=[ak])
                    S.op("act", lambda e, dst=dst, ACC=ACC: e.copy(dst, ACC), rd=[ak], wr=[dk])

            def out_proj_group(g4):
                wo = hy_w_out[g4 * 512:(g4 + 1) * 512, :].rearrange("(c p) n -> p c n", p=128)
                for piece in range(2):
                    view, sk = stream_w(wo[:, :, piece * 1024:(piece + 1) * 1024], 4, 1024)
                    for dcl in range(8):
                        dc = piece * 8 + dcl
                        for half in range(2):
                            b = opb[0] % 2
                            opb[0] += 1
                            for c in range(4):
                                S.op("pe", lambda e, view=view, c=c, dcl=dcl, half=half, b=b: e.matmul(
                                    bank(b), lhsT=view[:, c, dcl * 128:(dcl + 1) * 128],
                                    rhs=z2g[:, c, half * 512:(half + 1) * 512], start=(c == 0), stop=(c == 3)),
                                    rd=[sk, ("z2g", c)], wr=[PB(b)], tick=(c == 3))
                            S.op("dve", lambda e, dc=dc, half=half, b=b: e.scalar_tensor_tensor(
                                x[:, dc, half * 512:(half + 1) * 512], bank(b), mod(l, 2, dc),
                                x[:, dc, half * 512:(half + 1) * 512], ALU.mult, ALU.add),
                                rd=[PB(b), ("mod", l, 2), ("x", dc)], wr=[("x", dc)])

            stageA(0)
            for blk in range(16):
                gi = blk % 4
                s = blk % 2
                vc, x1c, x2c = vcb[s], x1cb[s], x2cb[s]
                vk, x1k, x2k = ("vc", s), ("x1c", s), ("x2c", s)
                cs = slice(blk * 128, (blk + 1) * 128)
                transpose_to_R(vc, vk)
                S.dma("pool", woutb[0:64, :, :], wff[:, :, cs], wr=["woutb"])
                S.dma("sp", winf, winf_v[:, :, cs], wr=["winf"])
                S.dma("sp", winb, winb_v[:, :, cs], wr=["winb"])
                for o in range(2):
                    for tc in range(8):
                        S.op("pe", lambda e, tc=tc, o=o: e.matmul(
                            bank(4 + tc // 2)[:, (tc % 2) * 256:(tc % 2) * 256 + 256], lhsT=h3b[0:64, tc * 128:(tc + 1) * 128],
                            rhs=woutb[0:64, 2 * o:2 * o + 2, :], start=True, stop=True),
                            rd=["h3b", "woutb"], wr=[PB(4 + tc // 2)], tick=(tc % 2 == 1))
                    F = ps[:, 2048:4096].rearrange("p (t d c) -> p t d c", d=2, c=128)
                    S.op("dve", lambda e, F=F: e.tensor_tensor(t1v, F[:, :, 0, :], winf, ALU.mult),
                         rd=[PB(4), PB(5), PB(6), PB(7), "winf"], wr=["t1"])
                    S.op("dve", lambda e, F=F: e.tensor_tensor(t2v, F[:, :, 1, :], winb, ALU.mult),
                         rd=[PB(4), PB(5), PB(6), PB(7), "winb"], wr=["t2"])
                    S.op("dve", lambda e: e.tensor_tensor(Rre[:, :, 0:128], t1v, t2v, ALU.add), rd=["t1", "t2"], wr=["RreK"])
                    S.op("dve", lambda e: e.tensor_tensor(Rim[:, :, 0:128], t1v, t2v, ALU.subtract), rd=["t1", "t2"], wr=["RimK"])
                    for part in range(2):
                        R = Rre if part == 0 else Rim
                        rk = ["RreK", "RreU"] if part == 0 else ["RimK", "RimU"]
                        for pc in range(2):
                            view, sk = stream_w(fwd_d[part * 2 + pc], 8, 512, q="sp", flat=True)
                            for fl in range(4):
                                fc = pc * 4 + fl
                                reg = bank(4 + fc // 2)[:, (fc % 2) * 256:(fc % 2) * 256 + 256]
                                for tc in range(8):
                                    S.op("pe", lambda e, view=view, fl=fl, tc=tc, reg=reg, R=R: e.matmul(
                                        reg, lhsT=view[:, tc, fl * 128:(fl + 1) * 128], rhs=R[:, tc, :],
                                        start=(tc == 0), stop=(tc == 7)),
                                        rd=[sk] + rk, wr=[PB(4 + fc // 2)], tick=(tc == 7))
                        src = ps[:, 2048:4096].rearrange("p (f c) -> p f c", c=256)
                        if part == 0:
                            S.op("act", lambda e, src=src: e.copy(SPre, src), rd=[PB(4), PB(5), PB(6), PB(7)], wr=["SPre"])
                        else:
                            S.op("act", lambda e, src=src: e.copy(SPim, src), rd=[PB(4), PB(5), PB(6), PB(7)], wr=["SPim"])
                    Kre, Ure = SPre[:, :, 0:128], SPre[:, :, 128:256]
                    Kim, Uim = SPim[:, :, 0:128], SPim[:, :, 128:256]
                    S.op("dve", lambda e: e.tensor_tensor(t1v, Ure, Kre, ALU.mult), rd=["SPre"], wr=["t1"])
                    S.op("dve", lambda e: e.tensor_tensor(t2v, Uim, Kim, ALU.mult), rd=["SPim"], wr=["t2"])
                    S.op("dve", lambda e: e.tensor_tensor(Y[:, 0:8, :], t1v, t2v, ALU.subtract), rd=["t1", "t2"], wr=["Yre"])
                    S.op("dve", lambda e: e.tensor_tensor(t1v, Ure, Kim, ALU.mult), rd=["SPre", "SPim"], wr=["t1"])
                    S.op("dve", lambda e: e.tensor_tensor(t2v, Uim, Kre, ALU.mult), rd=["SPre", "SPim"], wr=["t2"])
                    S.op("dve", lambda e: e.tensor_tensor(Y[:, 8:16, :], t1v, t2v, ALU.add), rd=["t1", "t2"], wr=["Yim"])
                    if blk + 1 < 16:
                        stageA(blk + 1, (0, 1) if o == 0 else (2,))
                    mod_run(1)
                    if o == 1 and gi == 0 and blk > 0:
                        out_proj_group(blk // 4 - 1)
                    for pc in range(4):
                        view, sk = stream_w(inv_d[pc], 16, 256, q="sp", flat=True)
                        reg = bank(2 + pc // 2)[:, (pc % 2) * 256:(pc % 2) * 256 + 256]
                        for fc in range(16):
                            S.op("pe", lambda e, view=view, fc=fc, reg=reg: e.matmul(
                                reg, lhsT=Y[:, fc, :], rhs=view[:, fc, :], start=(fc == 0), stop=(fc == 15)),
                                rd=[sk, "Yre", "Yim"], wr=[PB(2 + pc // 2)], tick=(fc == 15))
                    yps = ps[:, 1024:2048]
                    if o == 0:
                        S.op("act", lambda e: e.copy(t2, yps), rd=[PB(2), PB(3)], wr=["t2"])
                        S.op("act", lambda e, blk=blk, vc=vc: e.activation(t1, vc, AF.Identity, scale=pcol("hy_skip0", blk)),
                             rd=[vk, "pv"], wr=["t1"])
                        S.op("dve", lambda e: e.tensor_tensor(t1, t1, t2, ALU.add), rd=["t1", "t2"], wr=["t1"])
                        S.op("dve", lambda e, x1c=x1c: e.tensor_tensor(z1, t1, x1c, ALU.mult), rd=["t1", x1k], wr=["z1"])
                        transpose_to_R(z1, "z1")
                    else:
                        S.op("act", lambda e: e.copy(t2, yps), rd=[PB(2), PB(3)], wr=["t2"])
                        S.op("act", lambda e, blk=blk: e.activation(t1, z1, AF.Identity, scale=pcol("hy_skip1", blk)),
                             rd=["z1", "pv"], wr=["t1"])
                        S.op("dve", lambda e: e.tensor_tensor(t1, t1, t2, ALU.add), rd=["t1", "t2"], wr=["t1"])
                        S.op("dve", lambda e, gi=gi, x2c=x2c: e.tensor_tensor(z2g[:, gi, :], t1, x2c, ALU.mult),
                             rd=["t1", x2k], wr=[("z2g", gi)])
            out_proj_group(3)
            bo, _ = PV_OFF["hy_b_out"]
            S.op("dve", lambda e: e.tensor_tensor(gb, modT[l][:, 32:48], pv[:, bo:bo + 16], ALU.mult),
                 rd=[("mod", l, 2), "pv"], wr=["gb"])
            for dc in range(NDC):
                S.op("act", lambda e, dc=dc: e.activation(x[:, dc, :], x[:, dc, :], AF.Identity, bias=gb[:, dc:dc + 1]),
                     rd=[("x", dc), "gb"], wr=[("x", dc)])

        if use_hy:
            norm_mod(0, 0)
            hyena(0)
        norm_mod(0, 1)
        mlp(0)
        if use_ret:
            norm_mod(1, 0)
            retention(1)
            mod_bank[0] = 7
        norm_mod(1, 1)
        mlp(1)

        S.barrier()
        A.reset()
        tmpA = A.f32(2048)
        tmpB = A.f32(2048)
        rstd = A.f32(1024)
        rtmp = A.f32(1024)
        tmps = [(tmpA, "tmpA"), (tmpB, "tmpB")]
        compute_rstd(rstd, rtmp)
        fo, _ = PV_OFF["final_g"]
        for dc in range(NDC):
            S.op("dve", lambda e, dc=dc: e.scalar_tensor_tensor(
                x[:, dc, :], x[:, dc, :], pv[:, fo + dc:fo + dc + 1], rstd, ALU.mult, ALU.mult),
                rd=[("x", dc), "pv", "rstd"], wr=[("x", dc)])
        for tc in range(8):
            tm, tk = tmps[tc % 2]
            for g in range(4):
                b = (tc * 4 + g) % 8
                for j in range(4):
                    dc = g * 4 + j
                    S.op("pe", lambda e, b=b, j=j, dc=dc, tc=tc: e.transpose(
                        bank(b)[:, j * 128:(j + 1) * 128], x[:, dc, tc * 128:(tc + 1) * 128], ident[:, :]),
                        rd=[("x", dc), "ident"], wr=[PB(b)], tick=(j == 3))
                if g % 2 == 0:
                    S.op("act", lambda e, b=b, g=g, tm=tm: e.copy(tm[:, g * 512:(g + 1) * 512], bank(b)),
                         rd=[PB(b)], wr=[tk])
                else:
                    S.op("dve", lambda e, b=b, g=g, tm=tm: e.tensor_copy(tm[:, g * 512:(g + 1) * 512], bank(b)),
                         rd=[PB(b)], wr=[tk])
            S.dma("sp", y_out[tc * 128:(tc + 1) * 128, :], tm[:, :], rd=[tk])
        S.barrier()
        S.final_all("sp")

        with nc.Block() as block:
            S.emit(block)
    return nc


PERM = np.concatenate([np.arange(0, 256, 2), np.arange(1, 256, 2)])
MIN_DECAY = float(np.log(1e-2) / 1.5)
MAX_DECAY = float(np.log(1e-2) / 0.3)


def _core_consts(sample):
    L = 1024 if sample else 256
    nseq = T // L
    f32 = np.float32
    pos = np.arange(L, dtype=f32)
    t = (pos / f32(L)).astype(f32)
    omega = (f32(2.0 * np.pi) * pos / f32(L)).astype(f32)
    bands = np.linspace(1e-4, 15, 16, dtype=f32)
    ang = (omega[:, None] * bands[None, :]).astype(f32)
    feat = np.concatenate([t[:, None], np.cos(ang), -np.sin(ang)], axis=-1).astype(f32)
    featT = np.tile(feat.T, (1, nseq)).astype(f32)
    deltas = np.abs(np.linspace(MIN_DECAY, MAX_DECAY, D, dtype=f32))
    window = np.exp(-t[:, None] * deltas[None, :]).astype(f32)
    winb = window.copy()
    winb[0, :] = 0.0
    winf_t = np.tile(window, (nseq, 1))
    winb_t = np.tile(winb, (nseq, 1))
    n = 2 * L
    fi = np.arange(L, dtype=np.float64)
    ti = np.arange(L, dtype=np.float64)
    a = 2.0 * np.pi * (fi[None, :] + 0.5) * ti[:, None] / n
    C = np.cos(a)
    Sn = np.sin(a)
    fwd = np.zeros((T, 2 * T), np.float64)
    inv = np.zeros((2 * T, T), np.float64)
    for s in range(nseq):
        sl = slice(s * L, (s + 1) * L)
        fwd[sl, s * L:(s + 1) * L] = C
        fwd[sl, T + s * L:T + (s + 1) * L] = -Sn
        inv[s * L:(s + 1) * L, sl] = (2.0 / n) * C.T
        inv[T + s * L:T + (s + 1) * L, sl] = -(2.0 / n) * Sn.T
    rot = np.zeros((128, 2, T), f32)
    if sample:
        rows = np.repeat(np.arange(16, dtype=f32), 64)
        cols = np.tile(np.arange(64, dtype=f32), 16)
        invf = (f32(10000.0) ** (-np.arange(64, dtype=f32) / f32(64))).astype(f32)
        ang2 = np.concatenate([rows[:, None] * invf, cols[:, None] * invf], axis=-1).astype(f32)
        rot[:, 0, :] = np.cos(ang2).T
        rot[:, 1, :] = np.sin(ang2).T
    else:
        rot[:, 0, :] = 1.0
    j = np.arange(128, dtype=f32)[:, None]
    i = np.arange(128, dtype=f32)[None, :]
    rc = np.zeros((128, RC_N), f32)
    rc[:, 0:128] = np.maximum(i - j, 0)
    rc[:, 128:256] = (i >= j).astype(f32) / 16.0
    rc[:, 256:384] = np.maximum(j - i, 0)
    rc[:, 384:512] = (j >= i).astype(f32) / 16.0
    rc[:, 512:640] = np.broadcast_to(i + 1.0, (128, 128))
    rc[:, 640:768] = np.broadcast_to(128.0 - i, (128, 128))
    rc[:, 768] = 127.0 - j[:, 0]
    rc[:, 769] = j[:, 0]
    return dict(featT=featT, winf=winf_t.astype(f32), winb=winb_t.astype(f32),
                fwdtab=np.ascontiguousarray(fwd.reshape(8, 128, 4, 512).transpose(2, 1, 0, 3).reshape(4, 128, 4096)).astype(ml_dtypes.bfloat16), invtab=np.ascontiguousarray(inv.reshape(16, 128, 4, 256).transpose(2, 1, 0, 3).reshape(4, 128, 4096)).astype(ml_dtypes.bfloat16), rot=rot, rconst=rc)


def _pvec_fill(vals, bvals):
    pv = np.zeros((128, PV_N), np.float32)
    for name, v in vals.items():
        o, n = PV_OFF[name]
        v = np.asarray(v, np.float32).reshape(-1)
        if v.size < n * 128:
            v = np.concatenate([v, np.zeros(n * 128 - v.size, np.float32)])
        pv[:, o:o + n] = v.reshape(n, 128).T
    for name, v in bvals.items():
        o, n = PV_OFF[name]
        v = np.asarray(v, np.float32).reshape(-1)
        pv[:, o:o + v.size] = v[None, :]
    return pv


def make_in_maps(inputs, cores, stage=99):
    use_hy = stage in (3, 99)
    use_ret = stage in (2, 99)
    g = {k: np.asarray(v) for k, v in inputs.items()}
    ident = np.eye(128, dtype=np.float32)
    cc = {True: _core_consts(True), False: _core_consts(False)}
    shared = {"ident": ident, "w_ada": g["w_ada"], "mlp_w1": g["mlp_w1"], "mlp_w2": g["mlp_w2"]}
    if use_ret:
        wq = np.array(g["ret_w_qkvg"][0], dtype=np.float32, copy=True)
        for qi in range(2):
            for hd in range(8):
                c0 = qi * 2048 + hd * 256
                wq[:, c0:c0 + 256] = g["ret_w_qkvg"][0][:, c0 + PERM]
        shared["w_qkvg"] = wq
        shared["w_o"] = np.ascontiguousarray(g["ret_w_o"][0])
        shared["gng"] = np.ascontiguousarray(np.broadcast_to(g["ret_gn_g"][0][None, :], (128, 2 * D))).astype(np.float32)
    if use_hy:
        shared["hy_w_in"] = np.ascontiguousarray(g["hy_w_in"][0])
        shared["hy_w_out"] = np.ascontiguousarray(g["hy_w_out"][0])
        shared["hy_f_wout"] = np.ascontiguousarray(g["hy_f_wout"][0])
        shared["hy_f_w1"] = np.ascontiguousarray(g["hy_f_w1"][0])
        shared["hy_f_w2"] = np.ascontiguousarray(g["hy_f_w2"][0])
        shared["hy_f_w3"] = np.ascontiguousarray(g["hy_f_w3"][0])
    maps = []
    for c in cores:
        sample = c < 4
        if sample:
            xin = g["x_sample"][c]
            cond = g["c"][c]
        else:
            i = c - 4
            xin = g["x_prompt"][4 * i:4 * i + 4].reshape(T, D)
            cond = g["c_ctx"]
        vals = {"cond": cond, "final_g": g["final_g"],
                "hy_b_in": g["hy_b_in"][0], "hy_cb": g["hy_conv_b"][0],
                "hy_skip0": g["hy_f_skip"][0, 0], "hy_skip1": g["hy_f_skip"][0, 1], "hy_b_out": g["hy_b_out"][0]}
        for j in range(3):
            vals["hy_cw%d" % j] = g["hy_conv_w"][0, j]
            vals["hy_fb%d" % j] = (g["hy_f_b1"], g["hy_f_b2"], g["hy_f_b3"])[j][0]
            vals["hy_ff%d" % j] = g["hy_f_freq"][0, j]
        for l in range(2):
            vals["b_ada%d" % l] = g["b_ada"][l]
            vals["ng%d_0" % l] = g["norm_g"][l, 0]
            vals["ng%d_1" % l] = g["norm_g"][l, 1]
        fl = 1.0 if sample else 0.0
        bvals = {"ret_decay": g["ret_decay"][0].reshape(-1), "flags": np.array([fl, fl - 1.0], np.float32)}
        m = dict(shared)
        m["xin"] = np.ascontiguousarray(xin, dtype=np.float32)
        m["pvec"] = _pvec_fill(vals, bvals)
        k = cc[sample]
        if use_ret:
            if sample:
                st = np.ascontiguousarray(g["state_ret"][c, 0][:, :, PERM, :]).astype(np.float32)
            else:
                st = np.zeros((2, 8, 256, 512), np.float32)
            m["state_in"] = st
            m["rot"] = k["rot"]
            m["rconst"] = k["rconst"]
        if use_hy:
            for nm in ("featT", "winf", "winb", "fwdtab", "invtab"):
                m[nm] = k[nm]
        maps.append(m)
    return maps


_NC_CACHE = {}


def kernel(**inputs):
    if "nc" not in _NC_CACHE:
        _NC_CACHE["nc"] = build_nc()
    nc = _NC_CACHE["nc"]
    cores = list(range(8))
    in_maps = make_in_maps(inputs, cores)
    res = run_bass_kernel_spmd(nc, in_maps, core_ids=cores)
    ys = [np.asarray(r["y"]) for r in res.results]
    y_sample = np.stack(ys[0:4], axis=0).astype(np.float32)
    y_prompt = np.concatenate([ys[4 + i].reshape(4, 256, D) for i in range(4)], axis=0).astype(np.float32)
    new_state = np.zeros((16, 1, 2, 8, 256, 512), np.float32)
    for i in range(4):
        ns = np.asarray(res.results[4 + i]["nstate"]).reshape(4, 2, 8, 256, 512)
        new_state[4 * i:4 * i + 4, 0][:, :, :, PERM, :] = ns
    return (y_prompt, y_sample, new_state)
```
